# Optimizing a Trainium2 kernel written in Bass

```python
import jax, jax.numpy as jnp
from jax import lax
import numpy as np

D_MODEL = 1024
BATCH = 8
SEQ = 4096
DEPTH = 4

CHUNK = 64
N_META = 16
CONV_WIDTH = 3
D_CONV = D_MODEL
N_RET_HEADS = 4
RET_QK_DIM = 256
RET_V_DIM = 512
D_RET_QK = N_RET_HEADS * RET_QK_DIM
D_RET_V = N_RET_HEADS * RET_V_DIM
ROPE_BASE = 10000.0
RMS_EPS = 1e-6
GN_EPS = 1e-5
D_IN = 4 * D_CONV + 2 * D_RET_QK + 2 * D_RET_V + 2 * D_MODEL

kernel_name = "hybrid_shortconv_retention_metatoken_trunk"


def _rmsnorm(x, g):
    xf = x.astype(jnp.float32)
    y = xf * lax.rsqrt(jnp.mean(xf * xf, axis=-1, keepdims=True) + RMS_EPS)
    return (y * g.astype(jnp.float32)).astype(x.dtype)


def _rope(x, pos):
    half = x.shape[-1] // 2
    inv = ROPE_BASE ** (-jnp.arange(half, dtype=jnp.float32) / half)
    ang = pos[:, None] * inv[None, :]
    cos = jnp.cos(ang)[None, :, None, :]
    sin = jnp.sin(ang)[None, :, None, :]
    x1, x2 = x[..., :half], x[..., half:]
    return jnp.concatenate([x1 * cos - x2 * sin, x1 * sin + x2 * cos], axis=-1)


def _short_conv_branch(h, gb, gc, z, conv_w, conv_b, w_out):
    u = gc * h
    up = jnp.pad(u, ((0, 0), (CONV_WIDTH - 1, 0), (0, 0)))
    T = u.shape[1]
    y = conv_b + sum(conv_w[j] * up[:, j:j + T] for j in range(CONV_WIDTH))
    y = gb * y * jax.nn.silu(z)
    return y @ w_out


def _retention_branch(q, k, v, z, gn_g, w_out):
    Bsz, T, _ = q.shape
    H, dk, dv = N_RET_HEADS, RET_QK_DIM, RET_V_DIM
    pos = jnp.arange(T, dtype=jnp.float32)
    qf = _rope(q.astype(jnp.float32).reshape(Bsz, T, H, dk), pos)
    kf = _rope(k.astype(jnp.float32).reshape(Bsz, T, H, dk), pos) * (dk ** -0.5)
    vf = v.astype(jnp.float32).reshape(Bsz, T, H, dv)
    n_pad = (-N_META) % CHUNK
    padw = ((0, 0), (n_pad, 0), (0, 0), (0, 0))
    qf, kf, vf = (jnp.pad(a, padw) for a in (qf, kf, vf))
    L = T + n_pad
    n_chunks = L // CHUNK

    def to_chunks(a):
        return a.reshape(Bsz, n_chunks, CHUNK, H, a.shape[-1]).transpose(1, 0, 3, 2, 4)

    qc, kc, vc = to_chunks(qf), to_chunks(kf), to_chunks(vf)
    log_g = jnp.log(1.0 - 2.0 ** (-5.0 - jnp.arange(H, dtype=jnp.float32)))
    idx = jnp.arange(CHUNK, dtype=jnp.float32)
    intra_dec = jnp.exp(log_g[:, None, None] * jnp.abs(idx[:, None] - idx[None, :]))
    q_dec = jnp.exp(log_g[:, None] * (idx + 1.0))[None, :, :, None]
    k_dec = jnp.exp(log_g[:, None] * (CHUNK - 1.0 - idx))[None, :, :, None]
    chunk_dec = jnp.exp(log_g * CHUNK)[None, :, None, None]

    def step(state, inp):
        qi, ki, vi = inp
        s = jnp.einsum('bhnd,bhmd->bhnm', qi, ki) * intra_dec
        intra = jnp.einsum('bhnm,bhmv->bhnv', s, vi)
        cross = jnp.einsum('bhnd,bhdv->bhnv', qi * q_dec, state)
        new_state = state * chunk_dec + jnp.einsum('bhmd,bhmv->bhdv', ki * k_dec, vi)
        return new_state, intra + cross

    state0 = jnp.zeros((Bsz, H, dk, dv), jnp.float32)
    _, out = lax.scan(step, state0, (qc, kc, vc))
    out = out.transpose(1, 0, 3, 2, 4).reshape(Bsz, L, H, dv)[:, n_pad:]
    mu = jnp.mean(out, axis=-1, keepdims=True)
    var = jnp.mean(jnp.square(out - mu), axis=-1, keepdims=True)
    out = (out - mu) * lax.rsqrt(var + GN_EPS) * gn_g.astype(jnp.float32).reshape(H, dv)
    out = out.reshape(Bsz, T, D_RET_V).astype(z.dtype) * jax.nn.silu(z)
    return out @ w_out


def setup_inputs(seed: int = 0) -> dict:
    key = jax.random.key(seed)
    ks = jax.random.split(key, 16)
    nrm = jax.random.normal
    f32 = jnp.float32
    return {
        "x": nrm(ks[0], (BATCH, SEQ, D_MODEL), f32),
        "meta": nrm(ks[1], (N_META, D_MODEL), f32),
        "pre_norm_g": 1.0 + 0.05 * nrm(ks[2], (DEPTH, D_MODEL), f32),
        "w_in": nrm(ks[3], (DEPTH, D_MODEL, D_IN), f32) * D_MODEL ** -0.5,
        "b_in": 0.02 * nrm(ks[4], (DEPTH, D_IN), f32),
        "conv_w": nrm(ks[5], (DEPTH, CONV_WIDTH, D_CONV), f32) * CONV_WIDTH ** -0.5,
        "conv_b": 0.02 * nrm(ks[6], (DEPTH, D_CONV), f32),
        "w_conv_out": nrm(ks[7], (DEPTH, D_CONV, D_MODEL), f32) * D_CONV ** -0.5,
        "ret_gn_g": 1.0 + 0.05 * nrm(ks[8], (DEPTH, D_RET_V), f32),
        "w_ret_out": nrm(ks[9], (DEPTH, D_RET_V, D_MODEL), f32) * D_RET_V ** -0.5,
        "w_o": nrm(ks[10], (DEPTH, D_MODEL, D_MODEL), f32) * D_MODEL ** -0.5,
        "post_norm_g": 1.0 + 0.05 * nrm(ks[11], (DEPTH, D_MODEL), f32),
    }


def reference(x, meta, pre_norm_g, w_in, b_in, conv_w, conv_b, w_conv_out,
              ret_gn_g, w_ret_out, w_o, post_norm_g):
    Bsz = x.shape[0]
    meta_b = jnp.broadcast_to(meta[None].astype(x.dtype), (Bsz, N_META, D_MODEL))
    h = jnp.concatenate([meta_b, x], axis=1)
    sizes = (D_CONV,) * 4 + (D_RET_QK,) * 2 + (D_RET_V,) * 2 + (D_MODEL,) * 2
    split_at = [int(s) for s in np.cumsum(sizes)[:-1]]
    for l in range(DEPTH):
        xn = _rmsnorm(h, pre_norm_g[l])
        proj = xn @ w_in[l] + b_in[l]
        c_h, c_b, c_c, c_z, r_q, r_k, r_v, r_z, g_a, g_b = jnp.split(proj, split_at, axis=-1)
        y_a = _short_conv_branch(c_h, c_b, c_c, c_z, conv_w[l], conv_b[l], w_conv_out[l])
        y_b = _retention_branch(r_q, r_k, r_v, r_z, ret_gn_g[l], w_ret_out[l])
        y = jax.nn.sigmoid(g_a) * y_a + jax.nn.sigmoid(g_b) * y_b
        y = y @ w_o[l]
        h = h + _rmsnorm(y, post_norm_g[l])
    return h[:, N_META:]
```

```python
import numpy as np
from contextlib import ExitStack
import concourse.bass as bass
import concourse.mybir as mybir
from concourse.bass_utils import run_bass_kernel_spmd

F32 = mybir.dt.float32
BF16 = mybir.dt.bfloat16
AF = mybir.ActivationFunctionType
ALU = mybir.AluOpType

D = 1024
DIN = 12288
DEPTH = 4
NPAD = 112
NBLK = 33
TP = NBLK * 128
NH = 4
RMS_EPS = 1e-6
GN_EPS = 1e-5
NSLAB = 5
NBM = 3
TTM = NBM * 128
SLABS_PER_LAYER = 32

ENGS = ("sync", "scalar", "vector", "gpsimd", "tensor")
SAME_ENGINE_SYNC = {"sync": False, "scalar": True, "vector": True, "gpsimd": True, "tensor": False}
DMA_POOL = 8


def I(name, *a, **kw):
    return (name, a, kw)


class Res:
    __slots__ = ("name", "w", "r")

    def __init__(self, name=""):
        self.name = name
        self.w = []
        self.r = {}


class Op:
    __slots__ = ("eng", "fn", "deps", "semkey", "val", "inc", "dma", "phase")


class Sched:
    def __init__(self):
        self.ops = {e: [] for e in ENGS}
        self.phase = 0
        self.dma_hist = {e: [] for e in ENGS}

    def op(self, eng, fn, reads=(), writes=(), dma=False):
        o = Op()
        o.eng, o.fn, o.dma, o.phase = eng, fn, dma, self.phase
        o.inc = bool(dma)
        o.val = 0
        deps = {}
        for r in reads:
            for d in r.w:
                deps[id(d)] = d
        for w in writes:
            for d in w.w:
                deps[id(d)] = d
            for d in w.r.values():
                deps[id(d)] = d
        if dma:
            hist = self.dma_hist[eng]
            i = len(hist)
            o.semkey = ("dma", eng, i % DMA_POOL)
            o.val = 16 * (i // DMA_POOL + 1)
            if i >= DMA_POOL:
                d = hist[i - DMA_POOL]
                deps[id(d)] = d
            hist.append(o)
        else:
            o.semkey = ("eng", eng, self.phase)
        o.deps = list(deps.values())
        for r in reads:
            r.r[id(o) if dma else eng] = o
        for w in writes:
            w.w = [o]
            w.r = {}
        self.ops[eng].append(o)
        return o

    def switch(self, old, new):
        ev = {}
        for r in old:
            for d in r.w:
                ev[id(d)] = d
            for d in r.r.values():
                ev[id(d)] = d
        for r in new:
            r.w = []
            r.r = dict(ev)

    def _skip(self, o, d):
        return (not d.dma) and (not o.dma) and d.eng == o.eng and not SAME_ENGINE_SYNC[o.eng]

    def emit(self, nc):
        for e in ENGS:
            for o in self.ops[e]:
                for d in o.deps:
                    if d.dma or self._skip(o, d):
                        continue
                    d.inc = True
        keys = {}
        cnt = {}
        for e in ENGS:
            for o in self.ops[e]:
                keys[o.semkey] = None
                if not o.dma and o.inc:
                    cnt[o.semkey] = cnt.get(o.semkey, 0) + 1
                    o.val = cnt[o.semkey]
        self.maxval = max(list(cnt.values()) + [0])
        with ExitStack() as st:
            sems = {}
            for i, k in enumerate(keys):
                sems[k] = st.enter_context(nc.semaphore("s%d" % i))
            block = st.enter_context(nc.Block())

            def run(eng_name, e):
                seen = {}
                for o in self.ops[eng_name]:
                    waits = {}
                    for d in o.deps:
                        if self._skip(o, d):
                            continue
                        if waits.get(d.semkey, 0) < d.val:
                            waits[d.semkey] = d.val
                    for k, v in waits.items():
                        if seen.get(k, 0) >= v:
                            continue
                        e.wait_ge(sems[k], v)
                        seen[k] = v
                    if o.fn is not None:
                        name, a, kw = o.fn
                        ins = getattr(e, name)(*a, **kw)
                        if o.inc:
                            ins.then_inc(sems[o.semkey], 16 if o.dma else 1)

            @block.sync
            def _(e):
                run("sync", e)

            @block.scalar
            def _(e):
                run("scalar", e)

            @block.vector
            def _(e):
                run("vector", e)

            @block.gpsimd
            def _(e):
                run("gpsimd", e)

            @block.tensor
            def _(e):
                run("tensor", e)


def _head_g(h):
    return 1.0 - 2.0 ** (-5.0 - h)


def make_consts():
    inv = (np.float32(10000.0) ** (-(np.arange(128, dtype=np.float32)) / np.float32(128))).astype(np.float32)
    t = (np.arange(TP, dtype=np.float32) - np.float32(NPAD)).astype(np.float32)
    ang = (inv[:, None] * t[None, :]).astype(np.float32).astype(np.float64)
    cosT = np.cos(ang).astype(np.float32)
    sinT = np.sin(ang).astype(np.float32)
    n = np.arange(128)
    cn, nl = n // 64, n % 64
    scale = 1.0 / 16.0
    DT = np.zeros((128, NH, 128), np.float64)
    qdec = np.zeros((128, NH, 128), np.float64)
    kdec = np.zeros((128, NH), np.float64)
    for h in range(NH):
        g = _head_g(h)
        same = (cn[:, None] == cn[None, :])
        Dn_m = np.where(same, g ** np.abs(nl[:, None] - nl[None, :]).astype(np.float64), 0.0)
        cross = (cn[:, None] == 1) & (cn[None, :] == 0)
        Dn_m = np.where(cross, g ** ((nl[:, None] + 1) + (63 - nl[None, :])).astype(np.float64), Dn_m)
        DT[:, h, :] = scale * Dn_m.T
        qdec[:, h, :] = (g ** (n + 1.0))[None, :]
        kdec[:, h] = scale * g ** (127.0 - n)
    DT0 = DT.copy()
    DT0[:NPAD] = 0.0
    kdec0 = kdec.copy()
    kdec0[:NPAD] = 0.0
    return dict(
        cosT=cosT, sinT=sinT,
        DT=DT.reshape(128, NH * 128).astype(np.float32), DT0=DT0.reshape(128, NH * 128).astype(np.float32),
        qdec=qdec.reshape(128, NH * 128).astype(np.float32),
        kdec=kdec.astype(np.float32), kdec0=kdec0.astype(np.float32),
        ident=np.eye(128, dtype=np.float32),
    )


SLAB_TAGS = (["ch0", "cc0", "cb0", "cz0", "ch1", "cc1", "cb1", "cz1",
              "q0", "k0", "v00", "v01", "z0", "z1", "ga0", "ga1", "wco0", "wco1",
              "q1", "k1", "v10", "v11", "z2", "z3", "gb0", "gb1",
              "wroA0", "wroB0", "wroA1", "wroB1", "wo0", "wo1"])
GATE_ENG = "vector"
ROPE_ENG = "gpsimd"
IO_Q = "gpsimd"
ADD_ENG = "vector"


def build(L, fused):
    nc = bass.Bass("TRN2", target_bir_lowering=False)

    def dram(name, shape, dt, kind="ExternalInput"):
        return nc.dram_tensor(name, shape, dt, kind=kind).ap()

    h0 = dram("h0", [TP, D], F32)
    w_in = dram("w_in", [L, D, DIN], F32)
    w_co = dram("w_co", [L, D, D], F32)
    w_ro = dram("w_ro", [L, 2 * D, D], F32)
    w_o = dram("w_o", [L, D, D], F32)
    b_in = dram("b_in", [L, DIN], F32)
    bcol_d = dram("bcol", [L, 128, 96], F32)
    cw_d = dram("cw", [L, 128, 24], F32)
    cb_d = dram("cb", [L, 128, 8], F32)
    gng_d = dram("gng", [L, 128, 16], F32)
    preg_d = dram("preg", [L, 128, 8], F32)
    postg_d = dram("postg", [L, 128, D], F32)
    cos_d = dram("cosT", [128, TP], F32)
    sin_d = dram("sinT", [128, TP], F32)
    DT_d = dram("DT", [128, NH * 128], F32)
    DT0_d = dram("DT0", [128, NH * 128], F32)
    qdec_d = dram("qdec", [128, NH * 128], F32)
    kdec_d = dram("kdec", [128, NH], F32)
    kdec0_d = dram("kdec0", [128, NH], F32)
    ident_d = dram("ident", [128, 128], F32)
    wsc = dram("wsc", [L * SLABS_PER_LAYER, 128, 4096], BF16, kind="Internal")
    if fused:
        out_d = dram("out", [TP - 128, D], F32, kind="ExternalOutput")
        hbuf = dram("hbuf", [TP, D], F32, kind="Internal") if L > 1 else None
    else:
        out_d = dram("hout", [TP, D], F32, kind="ExternalOutput")
        hbuf = None

    S = Sched()
    sb_bytes = [0]
    with ExitStack() as st:
        def T(name, shape, dt):
            n = 1
            for s_ in shape[1:]:
                n *= s_
            sb_bytes[0] += n * (4 if dt == F32 else 2)
            return st.enter_context(nc.sbuf_tensor("s_" + name, shape, dt))

        slab = [T("slab%d" % i, [128, 8, 512], BF16) for i in range(NSLAB)]
        R_slab = [Res() for _ in range(NSLAB)]
        hblk = [T("hblk%d" % i, [128, D], F32) for i in range(3)]
        R_hblk = [Res() for _ in range(3)]
        xhat = [T("xhat%d" % i, [128, D], BF16) for i in range(4)]
        R_xhat = [Res() for _ in range(4)]
        gateT = T("gateT", [128, 8, TTM], BF16)
        R_gateT = [Res() for _ in range(8)]
        junk = T("junk", [128, D], BF16)
        R_junk = Res()
        xnT = T("xnT", [128, 8, TTM], BF16)
        R_xnT = [Res() for _ in range(4)]
        arenaX = T("arenaX", [128, 8 * (TTM + 2) + 8 * TTM], F32)
        cA = arenaX[:, 0:8 * (TTM + 2)].rearrange("p (f c) -> p f c", f=8)
        cB = arenaX[:, 8 * (TTM + 2):8 * (TTM + 2) + 8 * TTM].rearrange("p (f c) -> p f c", f=8)
        arenaXb = arenaX.bitcast(BF16)
        R_cA = [Res() for _ in range(8)]
        R_cB = [Res() for _ in range(8)]
        oT_off = 0
        oT = arenaXb[:, 0:16 * TTM].rearrange("p (f c) -> p f c", f=16)
        thT = arenaXb[:, 16 * TTM:24 * TTM].rearrange("p (f c) -> p f c", f=8)
        yT = arenaXb[:, 24 * TTM:32 * TTM].rearrange("p (f c) -> p f c", f=8)
        R_oT = [Res() for _ in range(16)]
        R_thT = [Res() for _ in range(8)]
        R_yT = [Res() for _ in range(8)]
        carry = T("carry", [128, 8, 2], F32)
        R_carry = Res()
        ycT = T("ycT", [128, 8, TTM], BF16)
        R_ycT = [Res() for _ in range(8)]
        sztmp = [T("sztmp%d" % i, [128, TTM], BF16) for i in range(2)]
        R_sztmp = [Res() for _ in range(2)]
        qT = T("qT", [128, 4, TTM], BF16)
        qdT = T("qdT", [128, 4, TTM], BF16)
        kT = T("kT", [128, 4, TTM], BF16)
        R_qT = [Res() for _ in range(4)]
        R_qdT = [Res() for _ in range(4)]
        R_kT = [Res() for _ in range(4)]
        rtmp = [T("rtmp%d" % i, [128, TTM], F32) for i in range(4)]
        R_rtmp = [Res() for _ in range(4)]
        kd = T("kd", [128, NBM, 512], BF16)
        R_kd = [Res() for _ in range(4)]
        vt = T("vt", [128, NBM, 1024], BF16)
        R_vt = [[Res() for _ in range(2)] for _ in range(4)]
        on = [T("on%d" % i, [128, 512], BF16) for i in range(4)]
        R_on = [Res() for _ in range(4)]
        STs = [T("ST%d" % i, [128, 256], BF16) for i in range(2)]
        R_ST = [Res() for _ in range(2)]
        state32 = T("state32", [128, NH, 2, 512], F32)
        state16 = T("state16", [128, 2 * NH, 2, 512], BF16)
        R_s32 = [[Res() for _ in range(2)] for _ in range(NH)]
        R_s16 = [[Res() for _ in range(2)] for _ in range(2 * NH)]
        ytmp = [T("ytmp%d" % i, [128, TTM], F32) for i in range(2)]
        R_ytmp = [Res() for _ in range(2)]
        fin = [T("fin%d" % i, [128, D], F32) for i in range(2)]
        R_fin = [Res() for _ in range(2)]
        hres, R_hres = hblk, R_hblk
        cos_t = T("cos_t", [128, TTM], F32)
        sin_t = T("sin_t", [128, TTM], F32)
        R_cs = Res()
        DT_t = T("DT_t", [128, NH * 128], F32)
        DT0_t = T("DT0_t", [128, NH * 128], F32)
        qdec_t = T("qdec_t", [128, NH * 128], F32)
        kdec_t = T("kdec_t", [128, NH], F32)
        kdec0_t = T("kdec0_t", [128, NH], F32)
        ident = T("ident", [128, 128], BF16)
        ones = T("ones", [1, 128], BF16)
        mhalf = T("mhalf", [128, 8], F32)
        R_const = Res()
        bcol = T("bcol", [128, 96], F32)
        hbcol = T("hbcol", [128, 96], F32)
        cw_t = T("cw_t", [128, 24], F32)
        cb_t = T("cb_t", [128, 8], F32)
        gng_t = T("gng_t", [128, 16], F32)
        preg_t = T("preg_t", [128, 8], F32)
        postg_t = T("postg_t", [128, D], F32)
        bvrow = T("bvrow", [1, 2048], BF16)
        R_lc = Res()
        R_hbcol = Res()
        ss = T("ss", [128, 8], F32)
        sd = T("sd", [128, 8], F32)
        rstd = T("rstd", [128, 8], F32)
        R_ss = [Res() for _ in range(8)]
        R_sd = [Res() for _ in range(8)]
        R_rstd = [Res() for _ in range(8)]
        st6 = [T("st6_%d" % i, [128, 2, 6], F32) for i in range(3)]
        mv = [T("mv%d" % i, [128, 2, 2], F32) for i in range(3)]
        gsd = [T("gsd%d" % i, [128, 2], F32) for i in range(3)]
        grs = [T("grs%d" % i, [128, 2], F32) for i in range(3)]
        R_gn = [Res() for _ in range(3)]
        R_gsd = [Res() for _ in range(3)]
        R_grs = [Res() for _ in range(3)]
        nmr = [T("nmr%d" % i, [128, 2], F32) for i in range(3)]
        R_nmr = [Res() for _ in range(3)]
        pregall = T("pregall", [128, L * 8], F32)
        NPM = 6
        pm = [st.enter_context(nc.psum_tensor("pm%d" % i, [128, 512], F32)) for i in range(NPM)]
        R_pm = [Res() for _ in range(NPM)]
        ptb = [st.enter_context(nc.psum_tensor("pt%d" % i, [128, 1024], BF16)) for i in range(2)]
        R_pt = [Res() for _ in range(2)]
        cnt = {"pm": 0, "pt": 0, "sz": 0, "on": 0, "st": 0, "yt": 0, "blk": 0, "blk2": 0, "gn": 0}

        def rot(key, n):
            i = cnt[key] % n
            cnt[key] += 1
            return i

        R_h = {}

        def rh(tag, b):
            k = (tag, b)
            if k not in R_h:
                R_h[k] = Res()
            return R_h[k]

        R_w16 = [[Res() for _ in range(SLABS_PER_LAYER)] for _ in range(L)]

        def slab_src(l, tag):
            def wi(c0):
                return w_in[l, :, c0:c0 + 512]
            k = tag[:-1] if tag[0] != "v" and not tag.startswith("wro") else tag
            i = int(tag[-1])
            if tag.startswith("ch"):
                return wi(0 + i * 512)
            if tag.startswith("cc"):
                return wi(2048 + i * 512)
            if tag.startswith("cb"):
                return wi(1024 + i * 512)
            if tag.startswith("cz"):
                return wi(3072 + i * 512)
            if tag.startswith("ga"):
                return wi(10240 + i * 512)
            if tag.startswith("gb"):
                return wi(11264 + i * 512)
            if tag.startswith("wco"):
                return w_co[l, :, i * 512:(i + 1) * 512]
            if tag.startswith("q"):
                return wi(4096 + i * 512)
            if tag.startswith("k"):
                return wi(5120 + i * 512)
            if tag.startswith("v"):
                hp, j = int(tag[1]), int(tag[2])
                return wi(6144 + hp * 1024 + j * 512)
            if tag.startswith("z"):
                return wi(8192 + i * 512)
            if tag.startswith("wroA"):
                return w_ro[l, 0:1024, i * 512:(i + 1) * 512]
            if tag.startswith("wroB"):
                return w_ro[l, 1024:2048, i * 512:(i + 1) * 512]
            if tag.startswith("wo"):
                return w_o[l, :, i * 512:(i + 1) * 512]
            raise ValueError(tag)

        def convert(l, s_, throttle=(), halves=(0, 1)):
            src = slab_src(l, SLAB_TAGS[s_]).rearrange("(kc p) c -> p kc c", p=128)
            dst = wsc[l * SLABS_PER_LAYER + s_].rearrange("p (kc c) -> p kc c", kc=8)
            for hf in halves:
                S.op("gpsimd", I("dma_start", out=dst[:, hf * 4:(hf + 1) * 4, :], in_=src[:, hf * 4:(hf + 1) * 4, :]),
                     reads=list(throttle), writes=[R_w16[l][s_]], dma=True)

        for s_ in range(SLABS_PER_LAYER):
            convert(0, s_)

        tiles = [(NBM * i, NBM) for i in range(NBLK // NBM)]
        assert NBLK % NBM == 0
        plan = [(l, s_) for l in range(L) for _ in tiles for s_ in range(SLABS_PER_LAYER)]
        sl = {"issued": 0, "next": 0}

        def get_slabs(l, tags):
            first = sl["next"]
            lim = min(len(plan), first + NSLAB)
            while sl["issued"] < lim:
                i = sl["issued"]
                pl, ps = plan[i]
                bi = i % NSLAB
                src = wsc[pl * SLABS_PER_LAYER + ps]
                dst = slab[bi][:].rearrange("p kc c -> p (kc c)")
                S.op("sync", I("dma_start", out=dst, in_=src),
                     reads=[R_w16[pl][ps]], writes=[R_slab[bi]], dma=True)
                sl["issued"] += 1
            outl = []
            for tg in tags:
                i = sl["next"]
                pl, ps = plan[i]
                assert pl == l and SLAB_TAGS[ps] == tg, (pl, l, SLAB_TAGS[ps], tg)
                outl.append((slab[i % NSLAB], R_slab[i % NSLAB]))
                sl["next"] += 1
            return outl

        for dst, src in ((DT_t, DT_d), (DT0_t, DT0_d), (qdec_t, qdec_d), (kdec_t, kdec_d), (kdec0_t, kdec0_d)):
            S.op("sync", I("dma_start", out=dst[:], in_=src), writes=[R_const], dma=True)
        S.op("gpsimd", I("dma_start", out=ident[:], in_=ident_d), writes=[R_const], dma=True)
        S.op("vector", I("memset", ones[:], 1.0), writes=[R_const])
        S.op("vector", I("memset", mhalf[:], -0.5), writes=[R_const])
        S.op("sync", I("dma_start", out=pregall[:].rearrange("p (l k) -> p l k", k=8), in_=preg_d.rearrange("l p k -> p l k")),
             writes=[R_const], dma=True)

        def bc(ap2d_tensor, col0, ncol, inner):
            a = ap2d_tensor[:, col0:col0 + ncol]
            return bass.AP(a.tensor, a.offset, [list(a.ap[0]), [1, ncol], [0, inner]])

        def mm_fm(slab_t, R_sl, jj, rhs_t, rhs_res, TT, nk=8, extra=None):
            bi = rot("pm", NPM)
            ps = pm[bi]
            chunks = [(slab_t, R_sl, kc) for kc in range(nk)]
            if extra is not None:
                chunks += [(extra[0], extra[1], kc) for kc in range(nk)]
            n = len(chunks)
            for ci, (sl_t, r_sl, kc) in enumerate(chunks):
                kk = ci
                S.op("tensor", I("matmul",
                    ps[:, 0:TT], lhsT=sl_t[:, kc, jj * 128:(jj + 1) * 128], rhs=rhs_t[:, kk, 0:TT],
                    start=(ci == 0), stop=(ci == n - 1)),
                    reads=[r_sl] + (rhs_res[kk] if isinstance(rhs_res[0], list) else rhs_res), writes=[R_pm[bi]])
            return ps, R_pm[bi]

        gt = [(l, b0, NB) for l in range(L) for (b0, NB) in tiles]
        pps = [[0], [0]]
        pp = [0]

        def src_of(l):
            return (h0, "h0") if l == 0 else (hbuf, "hbuf")

        a1buf = {}

        def do_A1_load(l, b0, NB):
            src_h, src_tag = src_of(l)
            for j in range(NB):
                b = b0 + j
                bi = rot("blk", 3)
                a1buf[(l, b)] = bi
                S.op(IO_Q, I("dma_start", out=hblk[bi][:], in_=src_h[b * 128:(b + 1) * 128, :]),
                     reads=[rh(src_tag, b)], writes=[R_hblk[bi]], dma=True)

        def do_A1(l, b0, NB):
            bis = []
            for j in range(NB):
                b = b0 + j
                bi = a1buf.pop((l, b))
                bis.append(bi)
                S.op("scalar", I("activation", out=junk[:], in_=hblk[bi][:], func=AF.Square, accum_out=ss[:, j:j + 1]),
                     reads=[R_hblk[bi]], writes=[R_junk, R_ss[j]])
            S.op("gpsimd", I("tensor_scalar", out=sd[:, 0:NB], in0=ss[:, 0:NB], scalar1=1.0 / D, scalar2=RMS_EPS, op0=ALU.mult, op1=ALU.add),
                 reads=R_ss[0:NB], writes=R_sd[0:NB])
            S.op("gpsimd", I("tensor_tensor", out=rstd[:, 0:NB], in0=sd[:, 0:NB], in1=mhalf[:, 0:NB], op=ALU.pow),
                 reads=R_sd[0:NB] + [R_const], writes=R_rstd[0:NB])
            for j in range(NB):
                bi = bis[j]
                S.op("gpsimd", I("tensor_scalar", out=xhat[j][:], in0=hblk[bi][:], scalar1=rstd[:, j:j + 1], scalar2=0.0,
                                 op0=ALU.mult, op1=ALU.add),
                     reads=[R_hblk[bi], R_rstd[j]], writes=[R_xhat[j]])

        def do_A2(l, b0, NB):
            for j in range(NB):
                xh = xhat[j]
                pi = rot("pt", 2)
                for kc in range(8):
                    S.op("tensor", I("transpose", out=ptb[pi][:, kc * 128:(kc + 1) * 128], in_=xh[:, kc * 128:(kc + 1) * 128],
                                     identity=ident[:]),
                         reads=[R_xhat[j], R_const], writes=[R_pt[pi]])
                S.op("vector", I("tensor_tensor", out=xnT[:, :, j * 128:(j + 1) * 128],
                                 in0=ptb[pi][:].rearrange("p (k c) -> p k c", k=8),
                                 in1=bc(pregall, l * 8, 8, 128), op=ALU.mult),
                     reads=[R_pt[pi], R_const], writes=[R_xnT[j]])

        def fm_items(l, tag, Rx, TT, evac):
            cur = {}

            def mk(jj):
                def item():
                    if jj == 0:
                        cur["s"] = get_slabs(l, [tag])[0]
                    sl_t, r_sl = cur["s"]
                    ps, rps = mm_fm(sl_t, r_sl, jj, xnT, Rx, TT)
                    evac(ps, rps, jj)
                return item
            return [mk(jj) for jj in range(4)]

        def core(l, hp, b0, NB, items):
            nz = 8
            per_rest = -(-(len(items) - nz) // max(NB - 1, 1))

            def st_block(j):
                b = b0 + j
                si = rot("pm", NPM)
                pss = pm[si]
                for hl in range(2):
                    for hf in range(2):
                        S.op("tensor", I("matmul", pss[:, hl * 128:(hl + 1) * 128], lhsT=kT[:, 2 * hl + hf, j * 128:(j + 1) * 128],
                                         rhs=qT[:, 2 * hl + hf, j * 128:(j + 1) * 128], start=(hf == 0), stop=(hf == 1)),
                             reads=[R_kT[2 * hl + hf], R_qT[2 * hl + hf]], writes=[R_pm[si]])
                sti = rot("st", 2)
                dtab = DT0_t if b == 0 else DT_t
                S.op("vector", I("tensor_tensor", out=STs[sti][:, :], in0=pss[:, 0:256], in1=dtab[:, hp * 256:(hp + 1) * 256], op=ALU.mult),
                     reads=[R_pm[si], R_const], writes=[R_ST[sti]])
                return sti

            sti_cur = st_block(0)
            pending = []
            for j in range(NB):
                b = b0 + j
                pin, pout = pp[0], 1 - pp[0]
                pp[0] = pout
                for hl in range(2):
                    h = 2 * hp + hl
                    for hf in range(2):
                        ui = rot("pm", NPM)
                        psu = pm[ui]
                        S.op("tensor", I("matmul", psu[:, :], lhsT=kd[:, j, hl * 256 + hf * 128:hl * 256 + (hf + 1) * 128],
                                         rhs=vt[:, j, hl * 512:(hl + 1) * 512], start=True, stop=True),
                             reads=[R_kd[j], R_vt[j][hl]], writes=[R_pm[ui]])
                        cdec = float(_head_g(h) ** 128)
                        S.op("vector", I("scalar_tensor_tensor", out=state32[:, h, hf, :], in0=state32[:, h, hf, :], scalar=cdec, in1=psu[:, :],
                                         op0=ALU.mult, op1=ALU.add),
                             reads=[R_pm[ui], R_s32[h][hf]], writes=[R_s32[h][hf]])
                        S.op("scalar", I("activation", out=state16[:, pout * NH + h, hf, :], in_=state32[:, h, hf, :], func=AF.Copy),
                             reads=[R_s32[h][hf]], writes=[R_s16[pout * NH + h][hf]])
                gi = rot("gn", 3)
                outs = []
                for hl in range(2):
                    h = 2 * hp + hl
                    oi = rot("pm", NPM)
                    pso = pm[oi]
                    for hf in range(2):
                        S.op("tensor", I("matmul", pso[:, :], lhsT=qdT[:, 2 * hl + hf, j * 128:(j + 1) * 128], rhs=state16[:, pin * NH + h, hf, :],
                                         start=(hf == 0), stop=False),
                             reads=[R_qdT[2 * hl + hf], R_s16[pin * NH + h][hf]], writes=[R_pm[oi]])
                    S.op("tensor", I("matmul", pso[:, :], lhsT=STs[sti_cur][:, hl * 128:(hl + 1) * 128], rhs=vt[:, j, hl * 512:(hl + 1) * 512],
                                     start=False, stop=True),
                         reads=[R_ST[sti_cur], R_vt[j][hl]], writes=[R_pm[oi]])
                    outs.append((oi, pso))
                for hl in range(2):
                    oi, pso = outs[hl]
                    S.op("vector", I("bn_stats", out=st6[gi][:, hl, :], in_=pso[:, :]), reads=[R_pm[oi]], writes=[R_gn[gi]])
                    S.op("vector", I("bn_aggr", out=mv[gi][:, hl, :], in_=st6[gi][:, hl, :]), reads=[R_gn[gi]], writes=[R_gn[gi]])
                S.op("gpsimd", I("tensor_scalar", out=gsd[gi][:, :], in0=mv[gi][:, :, 1], scalar1=1.0, scalar2=GN_EPS, op0=ALU.mult, op1=ALU.add),
                     reads=[R_gn[gi]], writes=[R_gsd[gi]])
                S.op("gpsimd", I("tensor_tensor", out=grs[gi][:, :], in0=gsd[gi][:, :], in1=mhalf[:, 0:2], op=ALU.pow),
                     reads=[R_gsd[gi], R_const], writes=[R_grs[gi]])
                if j + 1 < NB:
                    sti_next = st_block(j + 1)
                else:
                    sti_next = None
                S.op("vector", I("scalar_tensor_tensor", out=nmr[gi][:, :], in0=mv[gi][:, :, 0], scalar=-1.0, in1=grs[gi][:, :],
                                 op0=ALU.mult, op1=ALU.mult),
                     reads=[R_gn[gi], R_grs[gi]], writes=[R_nmr[gi]])
                nis = []
                for hl in range(2):
                    oi, pso = outs[hl]
                    ni = rot("on", 4)
                    S.op("scalar", I("activation", out=on[ni][:, :], in_=pso[:, :], func=AF.Identity, scale=grs[gi][:, hl:hl + 1],
                                     bias=nmr[gi][:, hl:hl + 1]),
                         reads=[R_pm[oi], R_grs[gi], R_nmr[gi]], writes=[R_on[ni]])
                    nis.append(ni)
                for _ in range(nz if j == 0 else per_rest):
                    if items:
                        items.pop(0)()
                if NB == 1:
                    while items:
                        items.pop(0)()
                if pending:
                    pending.pop(0)()

                def mk_tail(j=j, nis=nis):
                    def tail():
                        for hl in range(2):
                            h = 2 * hp + hl
                            ni = nis[hl]
                            pi = rot("pt", 2)
                            for c in range(4):
                                S.op("tensor", I("transpose", out=ptb[pi][:, c * 128:(c + 1) * 128], in_=on[ni][:, c * 128:(c + 1) * 128],
                                                 identity=ident[:]),
                                     reads=[R_on[ni], R_const], writes=[R_pt[pi]])
                            for c in range(4):
                                S.op("vector", I("scalar_tensor_tensor", out=oT[:, h * 4 + c, j * 128:(j + 1) * 128],
                                                 in0=ptb[pi][:, c * 128:(c + 1) * 128], scalar=gng_t[:, h * 4 + c:h * 4 + c + 1],
                                                 in1=gateT[:, hl * 4 + c, j * 128:(j + 1) * 128], op0=ALU.mult, op1=ALU.mult),
                                     reads=[R_pt[pi], R_lc, R_gateT[hl * 4 + c]], writes=[R_oT[h * 4 + c]])
                    return tail
                pending.append(mk_tail())
                sti_cur = sti_next
            while items:
                items.pop(0)()
            while pending:
                pending.pop(0)()

        do_A1_load(*gt[0])
        do_A1(*gt[0])
        do_A2(*gt[0])
        for g, (l, b0, NB) in enumerate(gt):
            nxt = gt[g + 1] if g + 1 < len(gt) else None
            first_of_layer = (b0 == 0)
            S.phase = l
            src_h, src_tag = src_of(l)
            last = (l == L - 1)
            if first_of_layer:
                for dst, src in ((bcol, bcol_d[l]), (cw_t, cw_d[l]), (cb_t, cb_d[l]), (gng_t, gng_d[l]), (postg_t, postg_d[l])):
                    S.op("sync", I("dma_start", out=dst[:], in_=src), writes=[R_lc], dma=True)
                S.op("gpsimd", I("dma_start", out=bvrow[:], in_=b_in[l:l + 1, 6144:8192]), writes=[R_lc], dma=True)
                S.op("vector", I("tensor_scalar", out=hbcol[:], in0=bcol[:], scalar1=0.5, scalar2=None, op0=ALU.mult),
                     reads=[R_lc], writes=[R_hbcol])
                for h in range(NH):
                    for hf in range(2):
                        S.op("vector", I("memset", state32[:, h, hf, :], 0.0), writes=[R_s32[h][hf]])
                        for q_ in range(2):
                            S.op("vector", I("memset", state16[:, q_ * NH + h, hf, :], 0.0), writes=[R_s16[q_ * NH + h][hf]])
                S.op("vector", I("memset", carry[:], 0.0), writes=[R_carry])
            if True:
                TT = NB * 128
                Rx = R_xnT[0:NB]
                ti = b0 // NBM

                tick = [0]

                def conv_tick(thr):
                    k = tick[0]
                    tick[0] += 1
                    s_ = (ti - 1) * 4 + k // 2
                    if l + 1 < L and ti >= 1 and s_ < SLABS_PER_LAYER and k < 8:
                        convert(l + 1, s_, throttle=[thr], halves=(k % 2,))

                conv_tick(R_xnT[0])

                if nxt is not None:
                    do_A1_load(*nxt)
                S.op("sync", I("dma_start", out=cos_t[:, 0:TT], in_=cos_d[:, b0 * 128:b0 * 128 + TT]), writes=[R_cs], dma=True)
                S.op("sync", I("dma_start", out=sin_t[:, 0:TT], in_=sin_d[:, b0 * 128:b0 * 128 + TT]), writes=[R_cs], dma=True)

                S.switch(R_oT + R_thT + R_yT, R_cA + R_cB)
                S.op("vector", I("tensor_copy", out=cA[:, :, 0:2], in_=carry[:]), reads=[R_carry], writes=R_cA)
                for i in range(2):
                    (sl_t, r_sl), = get_slabs(l, ["ch%d" % i])
                    for jj in range(4):
                        f = i * 4 + jj
                        ps, rps = mm_fm(sl_t, r_sl, jj, xnT, Rx, TT)
                        S.op("scalar", I("activation", out=cB[:, f, 0:TT], in_=ps[:, 0:TT], func=AF.Identity,
                                         bias=bcol[:, f:f + 1], scale=1.0),
                             reads=[rps, R_lc], writes=[R_cB[f]])
                    (sl_t, r_sl), = get_slabs(l, ["cc%d" % i])
                    for jj in range(4):
                        f = i * 4 + jj
                        ps, rps = mm_fm(sl_t, r_sl, jj, xnT, Rx, TT)
                        S.op("vector", I("scalar_tensor_tensor", out=cA[:, f, 2:2 + TT], in0=ps[:, 0:TT], scalar=bcol[:, 16 + f:17 + f],
                                         in1=cB[:, f, 0:TT], op0=ALU.add, op1=ALU.mult),
                             reads=[rps, R_lc, R_cB[f]], writes=[R_cA[f]])
                        if b0 == 0:
                            S.op("vector", I("memset", cA[:, f, 2:2 + NPAD], 0.0), writes=[R_cA[f]])
                        S.op("scalar", I("activation", out=cB[:, f, 0:TT], in_=cA[:, f, 2:2 + TT], func=AF.Identity,
                                         scale=cw_t[:, 16 + f:17 + f], bias=cb_t[:, f:f + 1]),
                             reads=[R_cA[f], R_lc], writes=[R_cB[f]])
                        S.op("vector", I("scalar_tensor_tensor", out=cB[:, f, 0:TT], in0=cA[:, f, 1:1 + TT], scalar=cw_t[:, 8 + f:9 + f],
                                         in1=cB[:, f, 0:TT], op0=ALU.mult, op1=ALU.add),
                             reads=[R_cA[f], R_lc, R_cB[f]], writes=[R_cB[f]])
                        S.op("vector", I("scalar_tensor_tensor", out=cB[:, f, 0:TT], in0=cA[:, f, 0:TT], scalar=cw_t[:, f:f + 1],
                                         in1=cB[:, f, 0:TT], op0=ALU.mult, op1=ALU.add),
                             reads=[R_cA[f], R_lc, R_cB[f]], writes=[R_cB[f]])
                    conv_tick(R_cs)
                    if i == 1:
                        S.op("vector", I("tensor_copy", out=carry[:], in_=cA[:, :, TT:TT + 2]), reads=R_cA, writes=[R_carry])
                    (sl_t, r_sl), = get_slabs(l, ["cb%d" % i])
                    for jj in range(4):
                        f = i * 4 + jj
                        ps, rps = mm_fm(sl_t, r_sl, jj, xnT, Rx, TT)
                        S.op("vector", I("scalar_tensor_tensor", out=cB[:, f, 0:TT], in0=ps[:, 0:TT], scalar=bcol[:, 8 + f:9 + f],
                                         in1=cB[:, f, 0:TT], op0=ALU.add, op1=ALU.mult),
                             reads=[rps, R_lc, R_cB[f]], writes=[R_cB[f]])
                    (sl_t, r_sl), = get_slabs(l, ["cz%d" % i])
                    for jj in range(4):
                        f = i * 4 + jj
                        ps, rps = mm_fm(sl_t, r_sl, jj, xnT, Rx, TT)
                        zi = rot("sz", 2)
                        S.op("scalar", I("activation", out=sztmp[zi][:, 0:TT], in_=ps[:, 0:TT], func=AF.Silu,
                                         bias=bcol[:, 24 + f:25 + f], scale=1.0),
                             reads=[rps, R_lc], writes=[R_sztmp[zi]])
                        S.op("vector", I("tensor_tensor", out=ycT[:, f, 0:TT], in0=cB[:, f, 0:TT], in1=sztmp[zi][:, 0:TT], op=ALU.mult),
                             reads=[R_cB[f], R_sztmp[zi]], writes=[R_ycT[f]])
                S.switch(R_cA + R_cB, R_oT + R_thT + R_yT)

                def z_evac(i):
                    def ev(ps, rps, jj):
                        f = i * 4 + jj
                        S.op("scalar", I("activation", out=gateT[:, f % 8, 0:TT], in_=ps[:, 0:TT], func=AF.Silu,
                                         bias=bcol[:, 64 + f:65 + f], scale=1.0),
                             reads=[rps, R_lc], writes=[R_gateT[f % 8]])
                    return ev

                def g_evac(i, base):
                    def ev(ps, rps, jj):
                        f = i * 4 + jj
                        S.op("scalar", I("activation", out=thT[:, f, 0:TT], in_=ps[:, 0:TT], func=AF.Tanh,
                                         bias=hbcol[:, base + f:base + f + 1], scale=0.5),
                             reads=[rps, R_hbcol], writes=[R_thT[f]])
                    return ev

                for hp in range(2):
                    for which in ("q", "k"):
                        (sl_t, r_sl), = get_slabs(l, ["%s%d" % (which, hp)])
                        dstT, R_dst = (qT, R_qT) if which == "q" else (kT, R_kT)
                        bbase = (32 if which == "q" else 40) + hp * 4
                        for hl in range(2):
                            ps1, rp1 = mm_fm(sl_t, r_sl, 2 * hl, xnT, Rx, TT)
                            ps2, rp2 = mm_fm(sl_t, r_sl, 2 * hl + 1, xnT, Rx, TT)
                            b1 = bcol[:, bbase + 2 * hl:bbase + 2 * hl + 1]
                            b2 = bcol[:, bbase + 2 * hl + 1:bbase + 2 * hl + 2]
                            for (pp, rp, bb, tab, ri) in ((ps1, rp1, b1, cos_t, 0), (ps2, rp2, b2, sin_t, 1),
                                                          (ps1, rp1, b1, sin_t, 2), (ps2, rp2, b2, cos_t, 3)):
                                S.op("vector", I("scalar_tensor_tensor", out=rtmp[ri][:, 0:TT], in0=pp[:, 0:TT], scalar=bb, in1=tab[:, 0:TT],
                                                 op0=ALU.add, op1=ALU.mult),
                                     reads=[rp, R_lc, R_cs], writes=[R_rtmp[ri]])
                            S.op(ROPE_ENG, I("tensor_tensor", out=dstT[:, 2 * hl, 0:TT], in0=rtmp[0][:, 0:TT], in1=rtmp[1][:, 0:TT], op=ALU.subtract),
                                 reads=[R_rtmp[0], R_rtmp[1]], writes=[R_dst[2 * hl]])
                            S.op(ROPE_ENG, I("tensor_tensor", out=dstT[:, 2 * hl + 1, 0:TT], in0=rtmp[2][:, 0:TT], in1=rtmp[3][:, 0:TT], op=ALU.add),
                                 reads=[R_rtmp[2], R_rtmp[3]], writes=[R_dst[2 * hl + 1]])
                            if which == "q":
                                h = 2 * hp + hl
                                qa = qdec_t[:, h * 128:(h + 1) * 128]
                                qb = bass.AP(qa.tensor, qa.offset, [list(qa.ap[0]), [0, NB], [1, 128]])
                                for hf in range(2):
                                    S.op("vector", I("tensor_tensor", out=qdT[:, 2 * hl + hf, 0:TT].rearrange("p (b c) -> p b c", c=128),
                                                     in0=qT[:, 2 * hl + hf, 0:TT].rearrange("p (b c) -> p b c", c=128), in1=qb, op=ALU.mult),
                                         reads=[R_qT[2 * hl + hf], R_const], writes=[R_qdT[2 * hl + hf]])
                    for hl in range(2):
                        h = 2 * hp + hl
                        (sl_t, r_sl), = get_slabs(l, ["v%d%d" % (hp, hl)])
                        for j in range(NB):
                            bi = rot("pm", NPM)
                            ps = pm[bi]
                            for kc in range(8):
                                S.op("tensor", I("matmul", ps[:, :], lhsT=xnT[:, kc, j * 128:(j + 1) * 128], rhs=sl_t[:, kc, :],
                                                 start=(kc == 0), stop=False),
                                     reads=[r_sl, R_xnT[j]], writes=[R_pm[bi]])
                            S.op("tensor", I("matmul", ps[:, :], lhsT=ones[0:1, :], rhs=bvrow[0:1, h * 512:(h + 1) * 512], start=False, stop=True),
                                 reads=[R_const, R_lc], writes=[R_pm[bi]])
                            S.op("scalar", I("activation", out=vt[:, j, hl * 512:(hl + 1) * 512], in_=ps[:, :], func=AF.Copy),
                                 reads=[R_pm[bi]], writes=[R_vt[j][hl]])
                    for j in range(NB):
                        pi = rot("pt", 2)
                        for c in range(4):
                            S.op("tensor", I("transpose", out=ptb[pi][:, c * 128:(c + 1) * 128], in_=kT[:, c, j * 128:(j + 1) * 128],
                                             identity=ident[:]),
                                 reads=[R_kT[c], R_const], writes=[R_pt[pi]])
                        kdt = kdec0_t if (b0 + j) == 0 else kdec_t
                        for hl in range(2):
                            h = 2 * hp + hl
                            S.op("scalar", I("activation", out=kd[:, j, hl * 256:(hl + 1) * 256], in_=ptb[pi][:, hl * 256:(hl + 1) * 256],
                                             func=AF.Copy, scale=kdt[:, h:h + 1]),
                                 reads=[R_pt[pi], R_const], writes=[R_kd[j]])
                    if hp == 0:
                        items = (fm_items(l, "z0", Rx, TT, z_evac(0)) + fm_items(l, "z1", Rx, TT, z_evac(1))
                                 + fm_items(l, "ga0", Rx, TT, g_evac(0, 80)) + fm_items(l, "ga1", Rx, TT, g_evac(1, 80)))
                    else:
                        if nxt is not None:
                            do_A1(*nxt)
                        items = (fm_items(l, "z2", Rx, TT, z_evac(2)) + fm_items(l, "z3", Rx, TT, z_evac(3))
                                 + fm_items(l, "gb0", Rx, TT, g_evac(0, 88)) + fm_items(l, "gb1", Rx, TT, g_evac(1, 88)))
                    conv_tick(R_kd[NB - 1])
                    pp = pps[hp]
                    core(l, hp, b0, NB, items)
                    conv_tick(R_oT[hp * 8 + 7])
                    if hp == 0:
                        for i in range(2):
                            (sl_t, r_sl), = get_slabs(l, ["wco%d" % i])
                            for jj in range(4):
                                f = i * 4 + jj
                                ps, rps = mm_fm(sl_t, r_sl, jj, ycT, [[r] for r in R_ycT], TT)
                                S.op("vector", I("scalar_tensor_tensor", out=yT[:, f, 0:TT], in0=thT[:, f, 0:TT], scalar=1.0, in1=ps[:, 0:TT],
                                                 op0=ALU.add, op1=ALU.mult),
                                     reads=[rps, R_thT[f]], writes=[R_yT[f]])
                if nxt is not None:
                    do_A2(*nxt)

                for c in range(2):
                    (slA, rA), (slB, rB) = get_slabs(l, ["wroA%d" % c, "wroB%d" % c])
                    for jj in range(4):
                        f = c * 4 + jj
                        ps, rps = mm_fm(slA, rA, jj, oT, [[r] for r in R_oT], TT, nk=8, extra=(slB, rB))
                        yi = rot("yt", 2)
                        S.op("vector", I("scalar_tensor_tensor", out=ytmp[yi][:, 0:TT], in0=thT[:, f, 0:TT], scalar=1.0, in1=ps[:, 0:TT],
                                         op0=ALU.add, op1=ALU.mult),
                             reads=[rps, R_thT[f]], writes=[R_ytmp[yi]])
                        S.op(ADD_ENG, I("tensor_tensor", out=yT[:, f, 0:TT], in0=yT[:, f, 0:TT], in1=ytmp[yi][:, 0:TT], op=ALU.add),
                             reads=[R_ytmp[yi], R_yT[f]], writes=[R_yT[f]])
                conv_tick(R_yT[7])
                (sl0, r0), (sl1, r1) = get_slabs(l, ["wo0", "wo1"])
                his = []
                for j in range(NB):
                    b = b0 + j
                    hi = rot("blk", 3)
                    his.append(hi)
                    S.op(IO_Q, I("dma_start", out=hres[hi][:], in_=src_h[b * 128:(b + 1) * 128, :]),
                         reads=[rh(src_tag, b)], writes=[R_hres[hi]], dma=True)
                for j in range(NB):
                    b = b0 + j
                    fi = rot("blk2", 2)
                    hi = his[j]
                    for c, (sl_t, r_sl) in enumerate(((sl0, r0), (sl1, r1))):
                        bi = rot("pm", NPM)
                        ps = pm[bi]
                        for kc in range(8):
                            S.op("tensor", I("matmul", ps[:, :], lhsT=yT[:, kc, j * 128:(j + 1) * 128], rhs=sl_t[:, kc, :],
                                             start=(kc == 0), stop=(kc == 7)),
                                 reads=[r_sl, R_yT[kc]], writes=[R_pm[bi]])
                        S.op("scalar", I("activation", out=fin[fi][:, c * 512:(c + 1) * 512], in_=ps[:, :], func=AF.Copy, scale=0.5),
                             reads=[R_pm[bi]], writes=[R_fin[fi]])
                    sj = 4 + j
                    S.op("scalar", I("activation", out=junk[:], in_=fin[fi][:], func=AF.Square, accum_out=ss[:, sj:sj + 1]),
                         reads=[R_fin[fi]], writes=[R_junk, R_ss[sj]])
                    S.op("gpsimd", I("tensor_scalar", out=sd[:, sj:sj + 1], in0=ss[:, sj:sj + 1], scalar1=1.0 / D, scalar2=RMS_EPS,
                                     op0=ALU.mult, op1=ALU.add),
                         reads=[R_ss[sj]], writes=[R_sd[sj]])
                    S.op("gpsimd", I("tensor_tensor", out=rstd[:, sj:sj + 1], in0=sd[:, sj:sj + 1], in1=mhalf[:, 0:1], op=ALU.pow),
                         reads=[R_sd[sj], R_const], writes=[R_rstd[sj]])
                    S.op("vector", I("scalar_tensor_tensor", out=fin[fi][:], in0=fin[fi][:], scalar=rstd[:, sj:sj + 1], in1=postg_t[:],
                                     op0=ALU.mult, op1=ALU.mult),
                         reads=[R_fin[fi], R_rstd[sj], R_lc], writes=[R_fin[fi]])
                    S.op(ADD_ENG, I("tensor_tensor", out=fin[fi][:], in0=fin[fi][:], in1=hres[hi][:], op=ALU.add),
                         reads=[R_fin[fi], R_hres[hi]], writes=[R_fin[fi]])
                    if fused and last:
                        if b >= 1:
                            S.op(IO_Q, I("dma_start", out=out_d[(b - 1) * 128:b * 128, :], in_=fin[fi][:]),
                                 reads=[R_fin[fi]], writes=[rh("out", b)], dma=True)
                    elif fused:
                        S.op(IO_Q, I("dma_start", out=hbuf[b * 128:(b + 1) * 128, :], in_=fin[fi][:]),
                             reads=[R_fin[fi]], writes=[rh("hbuf", b)], dma=True)
                    else:
                        S.op(IO_Q, I("dma_start", out=out_d[b * 128:(b + 1) * 128, :], in_=fin[fi][:]),
                             reads=[R_fin[fi]], writes=[rh("out", b)], dma=True)
        S.op("sync", None, reads=[r for (tag, b), r in R_h.items() if tag == "out"])
        assert sl["next"] == len(plan)
        S.emit(nc)
    nc._sb_bytes = sb_bytes[0]
    nc._maxsem = S.maxval
    nc._nops = {e: len(S.ops[e]) for e in ENGS}
    return nc


_CACHE = {}


def _get_nc(L, fused):
    key = (L, fused)
    if key not in _CACHE:
        _CACHE[key] = build(L, fused)
    return _CACHE[key]


def _layer_inputs(pre_norm_g, w_in, b_in, conv_w, conv_b, w_conv_out, ret_gn_g, w_ret_out, w_o, post_norm_g, ls):
    L = len(ls)
    f = np.float32
    ls = list(ls)
    return dict(
        w_in=np.ascontiguousarray(w_in[ls], dtype=f),
        w_co=np.ascontiguousarray(w_conv_out[ls], dtype=f),
        w_ro=np.ascontiguousarray(w_ret_out[ls], dtype=f),
        w_o=np.ascontiguousarray(w_o[ls], dtype=f),
        b_in=np.ascontiguousarray(b_in[ls], dtype=f),
        bcol=np.ascontiguousarray(b_in[ls].reshape(L, 96, 128).transpose(0, 2, 1), dtype=f),
        cw=np.ascontiguousarray(conv_w[ls].reshape(L, 3, 8, 128).transpose(0, 3, 1, 2).reshape(L, 128, 24), dtype=f),
        cb=np.ascontiguousarray(conv_b[ls].reshape(L, 8, 128).transpose(0, 2, 1), dtype=f),
        gng=np.ascontiguousarray(ret_gn_g[ls].reshape(L, 16, 128).transpose(0, 2, 1), dtype=f),
        preg=np.ascontiguousarray(pre_norm_g[ls].reshape(L, 8, 128).transpose(0, 2, 1), dtype=f),
        postg=np.ascontiguousarray(np.broadcast_to(post_norm_g[ls][:, None, :], (L, 128, D)), dtype=f),
    )


FUSED = True


def kernel(x, meta, pre_norm_g, w_in, b_in, conv_w, conv_b, w_conv_out, ret_gn_g, w_ret_out, w_o, post_norm_g):
    x = np.asarray(x, np.float32)
    meta = np.asarray(meta, np.float32)
    args = [np.asarray(a, np.float32) for a in (pre_norm_g, w_in, b_in, conv_w, conv_b, w_conv_out, ret_gn_g, w_ret_out, w_o, post_norm_g)]
    B = x.shape[0]
    consts = make_consts()
    h0s = []
    for b in range(B):
        h = np.zeros((TP, D), np.float32)
        h[NPAD:128] = meta
        h[128:] = x[b]
        h0s.append(h)
    if FUSED:
        nc = _get_nc(DEPTH, True)
        li = _layer_inputs(*args, ls=range(DEPTH))
        in_maps = [dict(h0=h0s[b], **li, **consts) for b in range(B)]
        res = run_bass_kernel_spmd(nc, in_maps, core_ids=list(range(B)))
        return np.stack([np.asarray(r["out"], np.float32) for r in res.results], axis=0)
    nc = _get_nc(1, False)
    hs = h0s
    for l in range(DEPTH):
        li = _layer_inputs(*args, ls=[l])
        in_maps = [dict(h0=hs[b], **li, **consts) for b in range(B)]
        res = run_bass_kernel_spmd(nc, in_maps, core_ids=list(range(B)))
        hs = [np.asarray(r["hout"], np.float32) for r in res.results]
    return np.stack([h[128:] for h in hs], axis=0)
```

```python
import numpy as np
from contextlib import ExitStack
import concourse.bass as bass
import concourse.mybir as mybir
from concourse.bass_utils import run_bass_kernel_spmd

F32 = mybir.dt.float32
BF16 = mybir.dt.bfloat16
AF = mybir.ActivationFunctionType
ALU = mybir.AluOpType

D = 1024
DIN = 12288
DEPTH = 4
NPAD = 112
NBLK = 33
TP = NBLK * 128
NH = 4
RMS_EPS = 1e-6
GN_EPS = 1e-5
NSLAB = 5
NBM = 3
TTM = NBM * 128
SLABS_PER_LAYER = 32

ENGS = ("sync", "scalar", "vector", "gpsimd", "tensor")
SAME_ENGINE_SYNC = {"sync": False, "scalar": True, "vector": True, "gpsimd": True, "tensor": False}
DMA_POOL = 8


def I(name, *a, **kw):
    return (name, a, kw)


class Res:
    __slots__ = ("name", "w", "r")

    def __init__(self, name=""):
        self.name = name
        self.w = []
        self.r = {}


class Op:
    __slots__ = ("eng", "fn", "deps", "semkey", "val", "inc", "dma", "phase")


class Sched:
    def __init__(self):
        self.ops = {e: [] for e in ENGS}
        self.phase = 0
        self.dma_hist = {e: [] for e in ENGS}

    def op(self, eng, fn, reads=(), writes=(), dma=False):
        o = Op()
        o.eng, o.fn, o.dma, o.phase = eng, fn, dma, self.phase
        o.inc = bool(dma)
        o.val = 0
        deps = {}
        for r in reads:
            for d in r.w:
                deps[id(d)] = d
        for w in writes:
            for d in w.w:
                deps[id(d)] = d
            for d in w.r.values():
                deps[id(d)] = d
        if dma:
            hist = self.dma_hist[eng]
            i = len(hist)
            o.semkey = ("dma", eng, i % DMA_POOL)
            o.val = 16 * (i // DMA_POOL + 1)
            if i >= DMA_POOL:
                d = hist[i - DMA_POOL]
                deps[id(d)] = d
            hist.append(o)
        else:
            o.semkey = ("eng", eng, self.phase)
        o.deps = list(deps.values())
        for r in reads:
            r.r[id(o) if dma else eng] = o
        for w in writes:
            w.w = [o]
            w.r = {}
        self.ops[eng].append(o)
        return o

    def switch(self, old, new):
        ev = {}
        for r in old:
            for d in r.w:
                ev[id(d)] = d
            for d in r.r.values():
                ev[id(d)] = d
        for r in new:
            r.w = []
            r.r = dict(ev)

    def _skip(self, o, d):
        return (not d.dma) and (not o.dma) and d.eng == o.eng and not SAME_ENGINE_SYNC[o.eng]

    def emit(self, nc):
        for e in ENGS:
            for o in self.ops[e]:
                for d in o.deps:
                    if d.dma or self._skip(o, d):
                        continue
                    d.inc = True
        keys = {}
        cnt = {}
        for e in ENGS:
            for o in self.ops[e]:
                keys[o.semkey] = None
                if not o.dma and o.inc:
                    cnt[o.semkey] = cnt.get(o.semkey, 0) + 1
                    o.val = cnt[o.semkey]
        self.maxval = max(list(cnt.values()) + [0])
        with ExitStack() as st:
            sems = {}
            for i, k in enumerate(keys):
                sems[k] = st.enter_context(nc.semaphore("s%d" % i))
            block = st.enter_context(nc.Block())

            def run(eng_name, e):
                seen = {}
                for o in self.ops[eng_name]:
                    waits = {}
                    for d in o.deps:
                        if self._skip(o, d):
                            continue
                        if waits.get(d.semkey, 0) < d.val:
                            waits[d.semkey] = d.val
                    for k, v in waits.items():
                        if seen.get(k, 0) >= v:
                            continue
                        e.wait_ge(sems[k], v)
                        seen[k] = v
                    if o.fn is not None:
                        name, a, kw = o.fn
                        ins = getattr(e, name)(*a, **kw)
                        if o.inc:
                            ins.then_inc(sems[o.semkey], 16 if o.dma else 1)

            @block.sync
            def _(e):
                run("sync", e)

            @block.scalar
            def _(e):
                run("scalar", e)

            @block.vector
            def _(e):
                run("vector", e)

            @block.gpsimd
            def _(e):
                run("gpsimd", e)

            @block.tensor
            def _(e):
                run("tensor", e)


def _head_g(h):
    return 1.0 - 2.0 ** (-5.0 - h)


def make_consts():
    inv = (np.float32(10000.0) ** (-(np.arange(128, dtype=np.float32)) / np.float32(128))).astype(np.float32)
    t = (np.arange(TP, dtype=np.float32) - np.float32(NPAD)).astype(np.float32)
    ang = (inv[:, None] * t[None, :]).astype(np.float32).astype(np.float64)
    cosT = np.cos(ang).astype(np.float32)
    sinT = np.sin(ang).astype(np.float32)
    n = np.arange(128)
    cn, nl = n // 64, n % 64
    scale = 1.0 / 16.0
    DT = np.zeros((128, NH, 128), np.float64)
    qdec = np.zeros((128, NH, 128), np.float64)
    kdec = np.zeros((128, NH), np.float64)
    for h in range(NH):
        g = _head_g(h)
        same = (cn[:, None] == cn[None, :])
        Dn_m = np.where(same, g ** np.abs(nl[:, None] - nl[None, :]).astype(np.float64), 0.0)
        cross = (cn[:, None] == 1) & (cn[None, :] == 0)
        Dn_m = np.where(cross, g ** ((nl[:, None] + 1) + (63 - nl[None, :])).astype(np.float64), Dn_m)
        DT[:, h, :] = scale * Dn_m.T
        qdec[:, h, :] = (g ** (n + 1.0))[None, :]
        kdec[:, h] = scale * g ** (127.0 - n)
    DT0 = DT.copy()
    DT0[:NPAD] = 0.0
    kdec0 = kdec.copy()
    kdec0[:NPAD] = 0.0
    return dict(
        cosT=cosT, sinT=sinT,
        DT=DT.reshape(128, NH * 128).astype(np.float32), DT0=DT0.reshape(128, NH * 128).astype(np.float32),
        qdec=qdec.reshape(128, NH * 128).astype(np.float32),
        kdec=kdec.astype(np.float32), kdec0=kdec0.astype(np.float32),
        ident=np.eye(128, dtype=np.float32),
    )


SLAB_TAGS = (["ch0", "cc0", "cb0", "cz0", "ch1", "cc1", "cb1", "cz1",
              "q0", "k0", "v00", "v01", "z0", "z1", "ga0", "ga1", "wco0", "wco1",
              "q1", "k1", "v10", "v11", "z2", "z3", "gb0", "gb1",
              "wroA0", "wroB0", "wroA1", "wroB1", "wo0", "wo1"])
GATE_ENG = "vector"
ROPE_ENG = "gpsimd"
IO_Q = "gpsimd"
ADD_ENG = "vector"


def build(L, fused):
    nc = bass.Bass("TRN2", target_bir_lowering=False)

    def dram(name, shape, dt, kind="ExternalInput"):
        return nc.dram_tensor(name, shape, dt, kind=kind).ap()

    h0 = dram("h0", [TP, D], F32)
    w_in = dram("w_in", [L, D, DIN], F32)
    w_co = dram("w_co", [L, D, D], F32)
    w_ro = dram("w_ro", [L, 2 * D, D], F32)
    w_o = dram("w_o", [L, D, D], F32)
    b_in = dram("b_in", [L, DIN], F32)
    bcol_d = dram("bcol", [L, 128, 96], F32)
    cw_d = dram("cw", [L, 128, 24], F32)
    cb_d = dram("cb", [L, 128, 8], F32)
    gng_d = dram("gng", [L, 128, 16], F32)
    preg_d = dram("preg", [L, 128, 8], F32)
    postg_d = dram("postg", [L, 128, D], F32)
    cos_d = dram("cosT", [128, TP], F32)
    sin_d = dram("sinT", [128, TP], F32)
    DT_d = dram("DT", [128, NH * 128], F32)
    DT0_d = dram("DT0", [128, NH * 128], F32)
    qdec_d = dram("qdec", [128, NH * 128], F32)
    kdec_d = dram("kdec", [128, NH], F32)
    kdec0_d = dram("kdec0", [128, NH], F32)
    ident_d = dram("ident", [128, 128], F32)
    wsc = dram("wsc", [L * SLABS_PER_LAYER, 128, 4096], BF16, kind="Internal")
    if fused:
        out_d = dram("out", [TP - 128, D], F32, kind="ExternalOutput")
        hbuf = dram("hbuf", [TP, D], F32, kind="Internal") if L > 1 else None
    else:
        out_d = dram("hout", [TP, D], F32, kind="ExternalOutput")
        hbuf = None

    S = Sched()
    sb_bytes = [0]
    with ExitStack() as st:
        def T(name, shape, dt):
            n = 1
            for s_ in shape[1:]:
                n *= s_
            sb_bytes[0] += n * (4 if dt == F32 else 2)
            return st.enter_context(nc.sbuf_tensor("s_" + name, shape, dt))

        slab = [T("slab%d" % i, [128, 8, 512], BF16) for i in range(NSLAB)]
        R_slab = [Res() for _ in range(NSLAB)]
        hblk = [T("hblk%d" % i, [128, D], F32) for i in range(3)]
        R_hblk = [Res() for _ in range(3)]
        xhat = [T("xhat%d" % i, [128, D], BF16) for i in range(4)]
        R_xhat = [Res() for _ in range(4)]
        gateT = T("gateT", [128, 8, TTM], BF16)
        R_gateT = [Res() for _ in range(8)]
        junk = T("junk", [128, D], BF16)
        R_junk = Res()
        xnT = T("xnT", [128, 8, TTM], BF16)
        R_xnT = [Res() for _ in range(4)]
        arenaX = T("arenaX", [128, 8 * (TTM + 2) + 8 * TTM], F32)
        cA = arenaX[:, 0:8 * (TTM + 2)].rearrange("p (f c) -> p f c", f=8)
        cB = arenaX[:, 8 * (TTM + 2):8 * (TTM + 2) + 8 * TTM].rearrange("p (f c) -> p f c", f=8)
        arenaXb = arenaX.bitcast(BF16)
        R_cA = [Res() for _ in range(8)]
        R_cB = [Res() for _ in range(8)]
        oT_off = 0
        oT = arenaXb[:, 0:16 * TTM].rearrange("p (f c) -> p f c", f=16)
        thT = arenaXb[:, 16 * TTM:24 * TTM].rearrange("p (f c) -> p f c", f=8)
        yT = arenaXb[:, 24 * TTM:32 * TTM].rearrange("p (f c) -> p f c", f=8)
        R_oT = [Res() for _ in range(16)]
        R_thT = [Res() for _ in range(8)]
        R_yT = [Res() for _ in range(8)]
        carry = T("carry", [128, 8, 2], F32)
        R_carry = Res()
        ycT = T("ycT", [128, 8, TTM], BF16)
        R_ycT = [Res() for _ in range(8)]
        sztmp = [T("sztmp%d" % i, [128, TTM], BF16) for i in range(2)]
        R_sztmp = [Res() for _ in range(2)]
        qT = T("qT", [128, 4, TTM], BF16)
        qdT = T("qdT", [128, 4, TTM], BF16)
        kT = T("kT", [128, 4, TTM], BF16)
        R_qT = [Res() for _ in range(4)]
        R_qdT = [Res() for _ in range(4)]
        R_kT = [Res() for _ in range(4)]
        rtmp = [T("rtmp%d" % i, [128, TTM], F32) for i in range(4)]
        R_rtmp = [Res() for _ in range(4)]
        kd = T("kd", [128, NBM, 512], BF16)
        R_kd = [Res() for _ in range(4)]
        vt = T("vt", [128, NBM, 1024], BF16)
        R_vt = [[Res() for _ in range(2)] for _ in range(4)]
        on = [T("on%d" % i, [128, 512], BF16) for i in range(4)]
        R_on = [Res() for _ in range(4)]
        STs = [T("ST%d" % i, [128, 256], BF16) for i in range(2)]
        R_ST = [Res() for _ in range(2)]
        state32 = T("state32", [128, NH, 2, 512], F32)
        state16 = T("state16", [128, 2 * NH, 2, 512], BF16)
        R_s32 = [[Res() for _ in range(2)] for _ in range(NH)]
        R_s16 = [[Res() for _ in range(2)] for _ in range(2 * NH)]
        ytmp = [T("ytmp%d" % i, [128, TTM], F32) for i in range(2)]
        R_ytmp = [Res() for _ in range(2)]
        fin = [T("fin%d" % i, [128, D], F32) for i in range(2)]
        R_fin = [Res() for _ in range(2)]
        hres, R_hres = hblk, R_hblk
        cos_t = T("cos_t", [128, TTM], F32)
        sin_t = T("sin_t", [128, TTM], F32)
        R_cs = Res()
        DT_t = T("DT_t", [128, NH * 128], F32)
        DT0_t = T("DT0_t", [128, NH * 128], F32)
        qdec_t = T("qdec_t", [128, NH * 128], F32)
        kdec_t = T("kdec_t", [128, NH], F32)
        kdec0_t = T("kdec0_t", [128, NH], F32)
        ident = T("ident", [128, 128], BF16)
        ones = T("ones", [1, 128], BF16)
        mhalf = T("mhalf", [128, 8], F32)
        R_const = Res()
        bcol = T("bcol", [128, 96], F32)
        hbcol = T("hbcol", [128, 96], F32)
        cw_t = T("cw_t", [128, 24], F32)
        cb_t = T("cb_t", [128, 8], F32)
        gng_t = T("gng_t", [128, 16], F32)
        preg_t = T("preg_t", [128, 8], F32)
        postg_t = T("postg_t", [128, D], F32)
        bvrow = T("bvrow", [1, 2048], BF16)
        R_lc = Res()
        R_hbcol = Res()
        ss = T("ss", [128, 8], F32)
        sd = T("sd", [128, 8], F32)
        rstd = T("rstd", [128, 8], F32)
        R_ss = [Res() for _ in range(8)]
        R_sd = [Res() for _ in range(8)]
        R_rstd = [Res() for _ in range(8)]
        st6 = [T("st6_%d" % i, [128, 2, 6], F32) for i in range(3)]
        mv = [T("mv%d" % i, [128, 2, 2], F32) for i in range(3)]
        gsd = [T("gsd%d" % i, [128, 2], F32) for i in range(3)]
        grs = [T("grs%d" % i, [128, 2], F32) for i in range(3)]
        R_gn = [Res() for _ in range(3)]
        R_gsd = [Res() for _ in range(3)]
        R_grs = [Res() for _ in range(3)]
        nmr = [T("nmr%d" % i, [128, 2], F32) for i in range(3)]
        R_nmr = [Res() for _ in range(3)]
        pregall = T("pregall", [128, L * 8], F32)
        NPM = 6
        pm = [st.enter_context(nc.psum_tensor("pm%d" % i, [128, 512], F32)) for i in range(NPM)]
        R_pm = [Res() for _ in range(NPM)]
        ptb = [st.enter_context(nc.psum_tensor("pt%d" % i, [128, 1024], BF16)) for i in range(2)]
        R_pt = [Res() for _ in range(2)]
        cnt = {"pm": 0, "pt": 0, "sz": 0, "on": 0, "st": 0, "yt": 0, "blk": 0, "blk2": 0, "gn": 0}

        def rot(key, n):
            i = cnt[key] % n
            cnt[key] += 1
            return i

        R_h = {}

        def rh(tag, b):
            k = (tag, b)
            if k not in R_h:
                R_h[k] = Res()
            return R_h[k]

        R_w16 = [[Res() for _ in range(SLABS_PER_LAYER)] for _ in range(L)]

        def slab_src(l, tag):
            def wi(c0):
                return w_in[l, :, c0:c0 + 512]
            k = tag[:-1] if tag[0] != "v" and not tag.startswith("wro") else tag
            i = int(tag[-1])
            if tag.startswith("ch"):
                return wi(0 + i * 512)
            if tag.startswith("cc"):
                return wi(2048 + i * 512)
            if tag.startswith("cb"):
                return wi(1024 + i * 512)
            if tag.startswith("cz"):
                return wi(3072 + i * 512)
            if tag.startswith("ga"):
                return wi(10240 + i * 512)
            if tag.startswith("gb"):
                return wi(11264 + i * 512)
            if tag.startswith("wco"):
                return w_co[l, :, i * 512:(i + 1) * 512]
            if tag.startswith("q"):
                return wi(4096 + i * 512)
            if tag.startswith("k"):
                return wi(5120 + i * 512)
            if tag.startswith("v"):
                hp, j = int(tag[1]), int(tag[2])
                return wi(6144 + hp * 1024 + j * 512)
            if tag.startswith("z"):
                return wi(8192 + i * 512)
            if tag.startswith("wroA"):
                return w_ro[l, 0:1024, i * 512:(i + 1) * 512]
            if tag.startswith("wroB"):
                return w_ro[l, 1024:2048, i * 512:(i + 1) * 512]
            if tag.startswith("wo"):
                return w_o[l, :, i * 512:(i + 1) * 512]
            raise ValueError(tag)

        def convert(l, s_, throttle=()):
            src = slab_src(l, SLAB_TAGS[s_]).rearrange("(kc p) c -> p kc c", p=128)
            dst = wsc[l * SLABS_PER_LAYER + s_].rearrange("p (kc c) -> p kc c", kc=8)
            S.op("gpsimd", I("dma_start", out=dst, in_=src), reads=list(throttle), writes=[R_w16[l][s_]], dma=True)

        for s_ in range(SLABS_PER_LAYER):
            convert(0, s_)

        tiles = [(NBM * i, NBM) for i in range(NBLK // NBM)]
        assert NBLK % NBM == 0
        plan = [(l, s_) for l in range(L) for _ in tiles for s_ in range(SLABS_PER_LAYER)]
        sl = {"issued": 0, "next": 0}

        def get_slabs(l, tags):
            first = sl["next"]
            lim = min(len(plan), first + NSLAB)
            while sl["issued"] < lim:
                i = sl["issued"]
                pl, ps = plan[i]
                bi = i % NSLAB
                src = wsc[pl * SLABS_PER_LAYER + ps]
                dst = slab[bi][:].rearrange("p kc c -> p (kc c)")
                S.op("sync", I("dma_start", out=dst, in_=src),
                     reads=[R_w16[pl][ps]], writes=[R_slab[bi]], dma=True)
                sl["issued"] += 1
            outl = []
            for tg in tags:
                i = sl["next"]
                pl, ps = plan[i]
                assert pl == l and SLAB_TAGS[ps] == tg, (pl, l, SLAB_TAGS[ps], tg)
                outl.append((slab[i % NSLAB], R_slab[i % NSLAB]))
                sl["next"] += 1
            return outl

        for dst, src in ((DT_t, DT_d), (DT0_t, DT0_d), (qdec_t, qdec_d), (kdec_t, kdec_d), (kdec0_t, kdec0_d)):
            S.op("sync", I("dma_start", out=dst[:], in_=src), writes=[R_const], dma=True)
        S.op("gpsimd", I("dma_start", out=ident[:], in_=ident_d), writes=[R_const], dma=True)
        S.op("vector", I("memset", ones[:], 1.0), writes=[R_const])
        S.op("vector", I("memset", mhalf[:], -0.5), writes=[R_const])
        S.op("sync", I("dma_start", out=pregall[:].rearrange("p (l k) -> p l k", k=8), in_=preg_d.rearrange("l p k -> p l k")),
             writes=[R_const], dma=True)

        def bc(ap2d_tensor, col0, ncol, inner):
            a = ap2d_tensor[:, col0:col0 + ncol]
            return bass.AP(a.tensor, a.offset, [list(a.ap[0]), [1, ncol], [0, inner]])

        def mm_fm(slab_t, R_sl, jj, rhs_t, rhs_res, TT, nk=8, extra=None):
            bi = rot("pm", NPM)
            ps = pm[bi]
            chunks = [(slab_t, R_sl, kc) for kc in range(nk)]
            if extra is not None:
                chunks += [(extra[0], extra[1], kc) for kc in range(nk)]
            n = len(chunks)
            for ci, (sl_t, r_sl, kc) in enumerate(chunks):
                kk = ci
                S.op("tensor", I("matmul",
                    ps[:, 0:TT], lhsT=sl_t[:, kc, jj * 128:(jj + 1) * 128], rhs=rhs_t[:, kk, 0:TT],
                    start=(ci == 0), stop=(ci == n - 1)),
                    reads=[r_sl] + (rhs_res[kk] if isinstance(rhs_res[0], list) else rhs_res), writes=[R_pm[bi]])
            return ps, R_pm[bi]

        gt = [(l, b0, NB) for l in range(L) for (b0, NB) in tiles]
        pps = [[0], [0]]
        pp = [0]

        def src_of(l):
            return (h0, "h0") if l == 0 else (hbuf, "hbuf")

        a1buf = {}

        def do_A1_load(l, b0, NB):
            src_h, src_tag = src_of(l)
            for j in range(NB):
                b = b0 + j
                bi = rot("blk", 3)
                a1buf[(l, b)] = bi
                S.op(IO_Q, I("dma_start", out=hblk[bi][:], in_=src_h[b * 128:(b + 1) * 128, :]),
                     reads=[rh(src_tag, b)], writes=[R_hblk[bi]], dma=True)

        def do_A1(l, b0, NB):
            bis = []
            for j in range(NB):
                b = b0 + j
                bi = a1buf.pop((l, b))
                bis.append(bi)
                S.op("scalar", I("activation", out=junk[:], in_=hblk[bi][:], func=AF.Square, accum_out=ss[:, j:j + 1]),
                     reads=[R_hblk[bi]], writes=[R_junk, R_ss[j]])
            S.op("gpsimd", I("tensor_scalar", out=sd[:, 0:NB], in0=ss[:, 0:NB], scalar1=1.0 / D, scalar2=RMS_EPS, op0=ALU.mult, op1=ALU.add),
                 reads=R_ss[0:NB], writes=R_sd[0:NB])
            S.op("gpsimd", I("tensor_tensor", out=rstd[:, 0:NB], in0=sd[:, 0:NB], in1=mhalf[:, 0:NB], op=ALU.pow),
                 reads=R_sd[0:NB] + [R_const], writes=R_rstd[0:NB])
            for j in range(NB):
                bi = bis[j]
                S.op("gpsimd", I("tensor_scalar", out=xhat[j][:], in0=hblk[bi][:], scalar1=rstd[:, j:j + 1], scalar2=0.0,
                                 op0=ALU.mult, op1=ALU.add),
                     reads=[R_hblk[bi], R_rstd[j]], writes=[R_xhat[j]])

        def do_A2(l, b0, NB):
            for j in range(NB):
                xh = xhat[j]
                pi = rot("pt", 2)
                for kc in range(8):
                    S.op("tensor", I("transpose", out=ptb[pi][:, kc * 128:(kc + 1) * 128], in_=xh[:, kc * 128:(kc + 1) * 128],
                                     identity=ident[:]),
                         reads=[R_xhat[j], R_const], writes=[R_pt[pi]])
                S.op("vector", I("tensor_tensor", out=xnT[:, :, j * 128:(j + 1) * 128],
                                 in0=ptb[pi][:].rearrange("p (k c) -> p k c", k=8),
                                 in1=bc(pregall, l * 8, 8, 128), op=ALU.mult),
                     reads=[R_pt[pi], R_const], writes=[R_xnT[j]])

        def fm_items(l, tag, Rx, TT, evac):
            cur = {}

            def mk(jj):
                def item():
                    if jj == 0:
                        cur["s"] = get_slabs(l, [tag])[0]
                    sl_t, r_sl = cur["s"]
                    ps, rps = mm_fm(sl_t, r_sl, jj, xnT, Rx, TT)
                    evac(ps, rps, jj)
                return item
            return [mk(jj) for jj in range(4)]

        def core(l, hp, b0, NB, items):
            nz = 8
            per_rest = -(-(len(items) - nz) // max(NB - 1, 1))

            def st_block(j):
                b = b0 + j
                si = rot("pm", NPM)
                pss = pm[si]
                for hl in range(2):
                    for hf in range(2):
                        S.op("tensor", I("matmul", pss[:, hl * 128:(hl + 1) * 128], lhsT=kT[:, 2 * hl + hf, j * 128:(j + 1) * 128],
                                         rhs=qT[:, 2 * hl + hf, j * 128:(j + 1) * 128], start=(hf == 0), stop=(hf == 1)),
                             reads=[R_kT[2 * hl + hf], R_qT[2 * hl + hf]], writes=[R_pm[si]])
                sti = rot("st", 2)
                dtab = DT0_t if b == 0 else DT_t
                S.op("vector", I("tensor_tensor", out=STs[sti][:, :], in0=pss[:, 0:256], in1=dtab[:, hp * 256:(hp + 1) * 256], op=ALU.mult),
                     reads=[R_pm[si], R_const], writes=[R_ST[sti]])
                return sti

            sti_cur = st_block(0)
            pending = []
            for j in range(NB):
                b = b0 + j
                pin, pout = pp[0], 1 - pp[0]
                pp[0] = pout
                for hl in range(2):
                    h = 2 * hp + hl
                    for hf in range(2):
                        ui = rot("pm", NPM)
                        psu = pm[ui]
                        S.op("tensor", I("matmul", psu[:, :], lhsT=kd[:, j, hl * 256 + hf * 128:hl * 256 + (hf + 1) * 128],
                                         rhs=vt[:, j, hl * 512:(hl + 1) * 512], start=True, stop=True),
                             reads=[R_kd[j], R_vt[j][hl]], writes=[R_pm[ui]])
                        cdec = float(_head_g(h) ** 128)
                        S.op("vector", I("scalar_tensor_tensor", out=state32[:, h, hf, :], in0=state32[:, h, hf, :], scalar=cdec, in1=psu[:, :],
                                         op0=ALU.mult, op1=ALU.add),
                             reads=[R_pm[ui], R_s32[h][hf]], writes=[R_s32[h][hf]])
                        S.op("scalar", I("activation", out=state16[:, pout * NH + h, hf, :], in_=state32[:, h, hf, :], func=AF.Copy),
                             reads=[R_s32[h][hf]], writes=[R_s16[pout * NH + h][hf]])
                gi = rot("gn", 3)
                outs = []
                for hl in range(2):
                    h = 2 * hp + hl
                    oi = rot("pm", NPM)
                    pso = pm[oi]
                    for hf in range(2):
                        S.op("tensor", I("matmul", pso[:, :], lhsT=qdT[:, 2 * hl + hf, j * 128:(j + 1) * 128], rhs=state16[:, pin * NH + h, hf, :],
                                         start=(hf == 0), stop=False),
                             reads=[R_qdT[2 * hl + hf], R_s16[pin * NH + h][hf]], writes=[R_pm[oi]])
                    S.op("tensor", I("matmul", pso[:, :], lhsT=STs[sti_cur][:, hl * 128:(hl + 1) * 128], rhs=vt[:, j, hl * 512:(hl + 1) * 512],
                                     start=False, stop=True),
                         reads=[R_ST[sti_cur], R_vt[j][hl]], writes=[R_pm[oi]])
                    outs.append((oi, pso))
                for hl in range(2):
                    oi, pso = outs[hl]
                    S.op("vector", I("bn_stats", out=st6[gi][:, hl, :], in_=pso[:, :]), reads=[R_pm[oi]], writes=[R_gn[gi]])
                    S.op("vector", I("bn_aggr", out=mv[gi][:, hl, :], in_=st6[gi][:, hl, :]), reads=[R_gn[gi]], writes=[R_gn[gi]])
                S.op("gpsimd", I("tensor_scalar", out=gsd[gi][:, :], in0=mv[gi][:, :, 1], scalar1=1.0, scalar2=GN_EPS, op0=ALU.mult, op1=ALU.add),
                     reads=[R_gn[gi]], writes=[R_gsd[gi]])
                S.op("gpsimd", I("tensor_tensor", out=grs[gi][:, :], in0=gsd[gi][:, :], in1=mhalf[:, 0:2], op=ALU.pow),
                     reads=[R_gsd[gi], R_const], writes=[R_grs[gi]])
                if j + 1 < NB:
                    sti_next = st_block(j + 1)
                else:
                    sti_next = None
                S.op("vector", I("scalar_tensor_tensor", out=nmr[gi][:, :], in0=mv[gi][:, :, 0], scalar=-1.0, in1=grs[gi][:, :],
                                 op0=ALU.mult, op1=ALU.mult),
                     reads=[R_gn[gi], R_grs[gi]], writes=[R_nmr[gi]])
                nis = []
                for hl in range(2):
                    oi, pso = outs[hl]
                    ni = rot("on", 4)
                    S.op("scalar", I("activation", out=on[ni][:, :], in_=pso[:, :], func=AF.Identity, scale=grs[gi][:, hl:hl + 1],
                                     bias=nmr[gi][:, hl:hl + 1]),
                         reads=[R_pm[oi], R_grs[gi], R_nmr[gi]], writes=[R_on[ni]])
                    nis.append(ni)
                for _ in range(nz if j == 0 else per_rest):
                    if items:
                        items.pop(0)()
                if NB == 1:
                    while items:
                        items.pop(0)()
                if pending:
                    pending.pop(0)()

                def mk_tail(j=j, nis=nis):
                    def tail():
                        for hl in range(2):
                            h = 2 * hp + hl
                            ni = nis[hl]
                            pi = rot("pt", 2)
                            for c in range(4):
                                S.op("tensor", I("transpose", out=ptb[pi][:, c * 128:(c + 1) * 128], in_=on[ni][:, c * 128:(c + 1) * 128],
                                                 identity=ident[:]),
                                     reads=[R_on[ni], R_const], writes=[R_pt[pi]])
                            for c in range(4):
                                S.op("vector", I("scalar_tensor_tensor", out=oT[:, h * 4 + c, j * 128:(j + 1) * 128],
                                                 in0=ptb[pi][:, c * 128:(c + 1) * 128], scalar=gng_t[:, h * 4 + c:h * 4 + c + 1],
                                                 in1=gateT[:, hl * 4 + c, j * 128:(j + 1) * 128], op0=ALU.mult, op1=ALU.mult),
                                     reads=[R_pt[pi], R_lc, R_gateT[hl * 4 + c]], writes=[R_oT[h * 4 + c]])
                    return tail
                pending.append(mk_tail())
                sti_cur = sti_next
            while items:
                items.pop(0)()
            while pending:
                pending.pop(0)()

        do_A1_load(*gt[0])
        do_A1(*gt[0])
        do_A2(*gt[0])
        for g, (l, b0, NB) in enumerate(gt):
            nxt = gt[g + 1] if g + 1 < len(gt) else None
            first_of_layer = (b0 == 0)
            S.phase = l
            src_h, src_tag = src_of(l)
            last = (l == L - 1)
            if first_of_layer:
                for dst, src in ((bcol, bcol_d[l]), (cw_t, cw_d[l]), (cb_t, cb_d[l]), (gng_t, gng_d[l]), (postg_t, postg_d[l])):
                    S.op("sync", I("dma_start", out=dst[:], in_=src), writes=[R_lc], dma=True)
                S.op("gpsimd", I("dma_start", out=bvrow[:], in_=b_in[l:l + 1, 6144:8192]), writes=[R_lc], dma=True)
                S.op("vector", I("tensor_scalar", out=hbcol[:], in0=bcol[:], scalar1=0.5, scalar2=None, op0=ALU.mult),
                     reads=[R_lc], writes=[R_hbcol])
                for h in range(NH):
                    for hf in range(2):
                        S.op("vector", I("memset", state32[:, h, hf, :], 0.0), writes=[R_s32[h][hf]])
                        for q_ in range(2):
                            S.op("vector", I("memset", state16[:, q_ * NH + h, hf, :], 0.0), writes=[R_s16[q_ * NH + h][hf]])
                S.op("vector", I("memset", carry[:], 0.0), writes=[R_carry])
            if True:
                TT = NB * 128
                Rx = R_xnT[0:NB]
                ti = b0 // NBM

                def conv_point(k, thr):
                    s_ = (ti - 1) * 4 + k
                    if l + 1 < L and ti >= 1 and s_ < SLABS_PER_LAYER:
                        convert(l + 1, s_, throttle=[thr])

                if nxt is not None:
                    do_A1_load(*nxt)
                S.op("sync", I("dma_start", out=cos_t[:, 0:TT], in_=cos_d[:, b0 * 128:b0 * 128 + TT]), writes=[R_cs], dma=True)
                S.op("sync", I("dma_start", out=sin_t[:, 0:TT], in_=sin_d[:, b0 * 128:b0 * 128 + TT]), writes=[R_cs], dma=True)

                S.switch(R_oT + R_thT + R_yT, R_cA + R_cB)
                S.op("vector", I("tensor_copy", out=cA[:, :, 0:2], in_=carry[:]), reads=[R_carry], writes=R_cA)
                for i in range(2):
                    (sl_t, r_sl), = get_slabs(l, ["ch%d" % i])
                    for jj in range(4):
                        f = i * 4 + jj
                        ps, rps = mm_fm(sl_t, r_sl, jj, xnT, Rx, TT)
                        S.op("scalar", I("activation", out=cB[:, f, 0:TT], in_=ps[:, 0:TT], func=AF.Identity,
                                         bias=bcol[:, f:f + 1], scale=1.0),
                             reads=[rps, R_lc], writes=[R_cB[f]])
                    (sl_t, r_sl), = get_slabs(l, ["cc%d" % i])
                    for jj in range(4):
                        f = i * 4 + jj
                        ps, rps = mm_fm(sl_t, r_sl, jj, xnT, Rx, TT)
                        S.op("vector", I("scalar_tensor_tensor", out=cA[:, f, 2:2 + TT], in0=ps[:, 0:TT], scalar=bcol[:, 16 + f:17 + f],
                                         in1=cB[:, f, 0:TT], op0=ALU.add, op1=ALU.mult),
                             reads=[rps, R_lc, R_cB[f]], writes=[R_cA[f]])
                        if b0 == 0:
                            S.op("vector", I("memset", cA[:, f, 2:2 + NPAD], 0.0), writes=[R_cA[f]])
                        S.op("scalar", I("activation", out=cB[:, f, 0:TT], in_=cA[:, f, 2:2 + TT], func=AF.Identity,
                                         scale=cw_t[:, 16 + f:17 + f], bias=cb_t[:, f:f + 1]),
                             reads=[R_cA[f], R_lc], writes=[R_cB[f]])
                        S.op("vector", I("scalar_tensor_tensor", out=cB[:, f, 0:TT], in0=cA[:, f, 1:1 + TT], scalar=cw_t[:, 8 + f:9 + f],
                                         in1=cB[:, f, 0:TT], op0=ALU.mult, op1=ALU.add),
                             reads=[R_cA[f], R_lc, R_cB[f]], writes=[R_cB[f]])
                        S.op("vector", I("scalar_tensor_tensor", out=cB[:, f, 0:TT], in0=cA[:, f, 0:TT], scalar=cw_t[:, f:f + 1],
                                         in1=cB[:, f, 0:TT], op0=ALU.mult, op1=ALU.add),
                             reads=[R_cA[f], R_lc, R_cB[f]], writes=[R_cB[f]])
                    if i == 1:
                        S.op("vector", I("tensor_copy", out=carry[:], in_=cA[:, :, TT:TT + 2]), reads=R_cA, writes=[R_carry])
                    (sl_t, r_sl), = get_slabs(l, ["cb%d" % i])
                    for jj in range(4):
                        f = i * 4 + jj
                        ps, rps = mm_fm(sl_t, r_sl, jj, xnT, Rx, TT)
                        S.op("vector", I("scalar_tensor_tensor", out=cB[:, f, 0:TT], in0=ps[:, 0:TT], scalar=bcol[:, 8 + f:9 + f],
                                         in1=cB[:, f, 0:TT], op0=ALU.add, op1=ALU.mult),
                             reads=[rps, R_lc, R_cB[f]], writes=[R_cB[f]])
                    (sl_t, r_sl), = get_slabs(l, ["cz%d" % i])
                    for jj in range(4):
                        f = i * 4 + jj
                        ps, rps = mm_fm(sl_t, r_sl, jj, xnT, Rx, TT)
                        zi = rot("sz", 2)
                        S.op("scalar", I("activation", out=sztmp[zi][:, 0:TT], in_=ps[:, 0:TT], func=AF.Silu,
                                         bias=bcol[:, 24 + f:25 + f], scale=1.0),
                             reads=[rps, R_lc], writes=[R_sztmp[zi]])
                        S.op("vector", I("tensor_tensor", out=ycT[:, f, 0:TT], in0=cB[:, f, 0:TT], in1=sztmp[zi][:, 0:TT], op=ALU.mult),
                             reads=[R_cB[f], R_sztmp[zi]], writes=[R_ycT[f]])
                S.switch(R_cA + R_cB, R_oT + R_thT + R_yT)

                def z_evac(i):
                    def ev(ps, rps, jj):
                        f = i * 4 + jj
                        S.op("scalar", I("activation", out=gateT[:, f % 8, 0:TT], in_=ps[:, 0:TT], func=AF.Silu,
                                         bias=bcol[:, 64 + f:65 + f], scale=1.0),
                             reads=[rps, R_lc], writes=[R_gateT[f % 8]])
                    return ev

                def g_evac(i, base):
                    def ev(ps, rps, jj):
                        f = i * 4 + jj
                        S.op("scalar", I("activation", out=thT[:, f, 0:TT], in_=ps[:, 0:TT], func=AF.Tanh,
                                         bias=hbcol[:, base + f:base + f + 1], scale=0.5),
                             reads=[rps, R_hbcol], writes=[R_thT[f]])
                    return ev

                for hp in range(2):
                    for which in ("q", "k"):
                        (sl_t, r_sl), = get_slabs(l, ["%s%d" % (which, hp)])
                        dstT, R_dst = (qT, R_qT) if which == "q" else (kT, R_kT)
                        bbase = (32 if which == "q" else 40) + hp * 4
                        for hl in range(2):
                            ps1, rp1 = mm_fm(sl_t, r_sl, 2 * hl, xnT, Rx, TT)
                            ps2, rp2 = mm_fm(sl_t, r_sl, 2 * hl + 1, xnT, Rx, TT)
                            b1 = bcol[:, bbase + 2 * hl:bbase + 2 * hl + 1]
                            b2 = bcol[:, bbase + 2 * hl + 1:bbase + 2 * hl + 2]
                            for (pp, rp, bb, tab, ri) in ((ps1, rp1, b1, cos_t, 0), (ps2, rp2, b2, sin_t, 1),
                                                          (ps1, rp1, b1, sin_t, 2), (ps2, rp2, b2, cos_t, 3)):
                                S.op("vector", I("scalar_tensor_tensor", out=rtmp[ri][:, 0:TT], in0=pp[:, 0:TT], scalar=bb, in1=tab[:, 0:TT],
                                                 op0=ALU.add, op1=ALU.mult),
                                     reads=[rp, R_lc, R_cs], writes=[R_rtmp[ri]])
                            S.op(ROPE_ENG, I("tensor_tensor", out=dstT[:, 2 * hl, 0:TT], in0=rtmp[0][:, 0:TT], in1=rtmp[1][:, 0:TT], op=ALU.subtract),
                                 reads=[R_rtmp[0], R_rtmp[1]], writes=[R_dst[2 * hl]])
                            S.op(ROPE_ENG, I("tensor_tensor", out=dstT[:, 2 * hl + 1, 0:TT], in0=rtmp[2][:, 0:TT], in1=rtmp[3][:, 0:TT], op=ALU.add),
                                 reads=[R_rtmp[2], R_rtmp[3]], writes=[R_dst[2 * hl + 1]])
                            if which == "q":
                                h = 2 * hp + hl
                                qa = qdec_t[:, h * 128:(h + 1) * 128]
                                qb = bass.AP(qa.tensor, qa.offset, [list(qa.ap[0]), [0, NB], [1, 128]])
                                for hf in range(2):
                                    S.op("vector", I("tensor_tensor", out=qdT[:, 2 * hl + hf, 0:TT].rearrange("p (b c) -> p b c", c=128),
                                                     in0=qT[:, 2 * hl + hf, 0:TT].rearrange("p (b c) -> p b c", c=128), in1=qb, op=ALU.mult),
                                         reads=[R_qT[2 * hl + hf], R_const], writes=[R_qdT[2 * hl + hf]])
                    for hl in range(2):
                        h = 2 * hp + hl
                        (sl_t, r_sl), = get_slabs(l, ["v%d%d" % (hp, hl)])
                        for j in range(NB):
                            bi = rot("pm", NPM)
                            ps = pm[bi]
                            for kc in range(8):
                                S.op("tensor", I("matmul", ps[:, :], lhsT=xnT[:, kc, j * 128:(j + 1) * 128], rhs=sl_t[:, kc, :],
                                                 start=(kc == 0), stop=False),
                                     reads=[r_sl, R_xnT[j]], writes=[R_pm[bi]])
                            S.op("tensor", I("matmul", ps[:, :], lhsT=ones[0:1, :], rhs=bvrow[0:1, h * 512:(h + 1) * 512], start=False, stop=True),
                                 reads=[R_const, R_lc], writes=[R_pm[bi]])
                            S.op("scalar", I("activation", out=vt[:, j, hl * 512:(hl + 1) * 512], in_=ps[:, :], func=AF.Copy),
                                 reads=[R_pm[bi]], writes=[R_vt[j][hl]])
                    for j in range(NB):
                        pi = rot("pt", 2)
                        for c in range(4):
                            S.op("tensor", I("transpose", out=ptb[pi][:, c * 128:(c + 1) * 128], in_=kT[:, c, j * 128:(j + 1) * 128],
                                             identity=ident[:]),
                                 reads=[R_kT[c], R_const], writes=[R_pt[pi]])
                        kdt = kdec0_t if (b0 + j) == 0 else kdec_t
                        for hl in range(2):
                            h = 2 * hp + hl
                            S.op("scalar", I("activation", out=kd[:, j, hl * 256:(hl + 1) * 256], in_=ptb[pi][:, hl * 256:(hl + 1) * 256],
                                             func=AF.Copy, scale=kdt[:, h:h + 1]),
                                 reads=[R_pt[pi], R_const], writes=[R_kd[j]])
                    if hp == 0:
                        items = (fm_items(l, "z0", Rx, TT, z_evac(0)) + fm_items(l, "z1", Rx, TT, z_evac(1))
                                 + fm_items(l, "ga0", Rx, TT, g_evac(0, 80)) + fm_items(l, "ga1", Rx, TT, g_evac(1, 80)))
                    else:
                        if nxt is not None:
                            do_A1(*nxt)
                        items = (fm_items(l, "z2", Rx, TT, z_evac(2)) + fm_items(l, "z3", Rx, TT, z_evac(3))
                                 + fm_items(l, "gb0", Rx, TT, g_evac(0, 88)) + fm_items(l, "gb1", Rx, TT, g_evac(1, 88)))
                    conv_point(2 * hp, R_kd[0])
                    conv_point(2 * hp + 1, R_kd[NB - 1])
                    pp = pps[hp]
                    core(l, hp, b0, NB, items)
                    if hp == 0:
                        for i in range(2):
                            (sl_t, r_sl), = get_slabs(l, ["wco%d" % i])
                            for jj in range(4):
                                f = i * 4 + jj
                                ps, rps = mm_fm(sl_t, r_sl, jj, ycT, [[r] for r in R_ycT], TT)
                                S.op("vector", I("scalar_tensor_tensor", out=yT[:, f, 0:TT], in0=thT[:, f, 0:TT], scalar=1.0, in1=ps[:, 0:TT],
                                                 op0=ALU.add, op1=ALU.mult),
                                     reads=[rps, R_thT[f]], writes=[R_yT[f]])
                if nxt is not None:
                    do_A2(*nxt)

                for c in range(2):
                    (slA, rA), (slB, rB) = get_slabs(l, ["wroA%d" % c, "wroB%d" % c])
                    for jj in range(4):
                        f = c * 4 + jj
                        ps, rps = mm_fm(slA, rA, jj, oT, [[r] for r in R_oT], TT, nk=8, extra=(slB, rB))
                        yi = rot("yt", 2)
                        S.op("vector", I("scalar_tensor_tensor", out=ytmp[yi][:, 0:TT], in0=thT[:, f, 0:TT], scalar=1.0, in1=ps[:, 0:TT],
                                         op0=ALU.add, op1=ALU.mult),
                             reads=[rps, R_thT[f]], writes=[R_ytmp[yi]])
                        S.op(ADD_ENG, I("tensor_tensor", out=yT[:, f, 0:TT], in0=yT[:, f, 0:TT], in1=ytmp[yi][:, 0:TT], op=ALU.add),
                             reads=[R_ytmp[yi], R_yT[f]], writes=[R_yT[f]])
                (sl0, r0), (sl1, r1) = get_slabs(l, ["wo0", "wo1"])
                his = []
                for j in range(NB):
                    b = b0 + j
                    hi = rot("blk", 3)
                    his.append(hi)
                    S.op(IO_Q, I("dma_start", out=hres[hi][:], in_=src_h[b * 128:(b + 1) * 128, :]),
                         reads=[rh(src_tag, b)], writes=[R_hres[hi]], dma=True)
                for j in range(NB):
                    b = b0 + j
                    fi = rot("blk2", 2)
                    hi = his[j]
                    for c, (sl_t, r_sl) in enumerate(((sl0, r0), (sl1, r1))):
                        bi = rot("pm", NPM)
                        ps = pm[bi]
                        for kc in range(8):
                            S.op("tensor", I("matmul", ps[:, :], lhsT=yT[:, kc, j * 128:(j + 1) * 128], rhs=sl_t[:, kc, :],
                                             start=(kc == 0), stop=(kc == 7)),
                                 reads=[r_sl, R_yT[kc]], writes=[R_pm[bi]])
                        S.op("scalar", I("activation", out=fin[fi][:, c * 512:(c + 1) * 512], in_=ps[:, :], func=AF.Copy, scale=0.5),
                             reads=[R_pm[bi]], writes=[R_fin[fi]])
                    sj = 4 + j
                    S.op("scalar", I("activation", out=junk[:], in_=fin[fi][:], func=AF.Square, accum_out=ss[:, sj:sj + 1]),
                         reads=[R_fin[fi]], writes=[R_junk, R_ss[sj]])
                    S.op("gpsimd", I("tensor_scalar", out=sd[:, sj:sj + 1], in0=ss[:, sj:sj + 1], scalar1=1.0 / D, scalar2=RMS_EPS,
                                     op0=ALU.mult, op1=ALU.add),
                         reads=[R_ss[sj]], writes=[R_sd[sj]])
                    S.op("gpsimd", I("tensor_tensor", out=rstd[:, sj:sj + 1], in0=sd[:, sj:sj + 1], in1=mhalf[:, 0:1], op=ALU.pow),
                         reads=[R_sd[sj], R_const], writes=[R_rstd[sj]])
                    S.op("vector", I("scalar_tensor_tensor", out=fin[fi][:], in0=fin[fi][:], scalar=rstd[:, sj:sj + 1], in1=postg_t[:],
                                     op0=ALU.mult, op1=ALU.mult),
                         reads=[R_fin[fi], R_rstd[sj], R_lc], writes=[R_fin[fi]])
                    S.op(ADD_ENG, I("tensor_tensor", out=fin[fi][:], in0=fin[fi][:], in1=hres[hi][:], op=ALU.add),
                         reads=[R_fin[fi], R_hres[hi]], writes=[R_fin[fi]])
                    if fused and last:
                        if b >= 1:
                            S.op(IO_Q, I("dma_start", out=out_d[(b - 1) * 128:b * 128, :], in_=fin[fi][:]),
                                 reads=[R_fin[fi]], writes=[rh("out", b)], dma=True)
                    elif fused:
                        S.op(IO_Q, I("dma_start", out=hbuf[b * 128:(b + 1) * 128, :], in_=fin[fi][:]),
                             reads=[R_fin[fi]], writes=[rh("hbuf", b)], dma=True)
                    else:
                        S.op(IO_Q, I("dma_start", out=out_d[b * 128:(b + 1) * 128, :], in_=fin[fi][:]),
                             reads=[R_fin[fi]], writes=[rh("out", b)], dma=True)
        S.op("sync", None, reads=[r for (tag, b), r in R_h.items() if tag == "out"])
        assert sl["next"] == len(plan)
        S.emit(nc)
    nc._sb_bytes = sb_bytes[0]
    nc._maxsem = S.maxval
    nc._nops = {e: len(S.ops[e]) for e in ENGS}
    return nc


_CACHE = {}


def _get_nc(L, fused):
    key = (L, fused)
    if key not in _CACHE:
        _CACHE[key] = build(L, fused)
    return _CACHE[key]


def _layer_inputs(pre_norm_g, w_in, b_in, conv_w, conv_b, w_conv_out, ret_gn_g, w_ret_out, w_o, post_norm_g, ls):
    L = len(ls)
    f = np.float32
    ls = list(ls)
    return dict(
        w_in=np.ascontiguousarray(w_in[ls], dtype=f),
        w_co=np.ascontiguousarray(w_conv_out[ls], dtype=f),
        w_ro=np.ascontiguousarray(w_ret_out[ls], dtype=f),
        w_o=np.ascontiguousarray(w_o[ls], dtype=f),
        b_in=np.ascontiguousarray(b_in[ls], dtype=f),
        bcol=np.ascontiguousarray(b_in[ls].reshape(L, 96, 128).transpose(0, 2, 1), dtype=f),
        cw=np.ascontiguousarray(conv_w[ls].reshape(L, 3, 8, 128).transpose(0, 3, 1, 2).reshape(L, 128, 24), dtype=f),
        cb=np.ascontiguousarray(conv_b[ls].reshape(L, 8, 128).transpose(0, 2, 1), dtype=f),
        gng=np.ascontiguousarray(ret_gn_g[ls].reshape(L, 16, 128).transpose(0, 2, 1), dtype=f),
        preg=np.ascontiguousarray(pre_norm_g[ls].reshape(L, 8, 128).transpose(0, 2, 1), dtype=f),
        postg=np.ascontiguousarray(np.broadcast_to(post_norm_g[ls][:, None, :], (L, 128, D)), dtype=f),
    )


FUSED = True


def kernel(x, meta, pre_norm_g, w_in, b_in, conv_w, conv_b, w_conv_out, ret_gn_g, w_ret_out, w_o, post_norm_g):
    x = np.asarray(x, np.float32)
    meta = np.asarray(meta, np.float32)
    args = [np.asarray(a, np.float32) for a in (pre_norm_g, w_in, b_in, conv_w, conv_b, w_conv_out, ret_gn_g, w_ret_out, w_o, post_norm_g)]
    B = x.shape[0]
    consts = make_consts()
    h0s = []
    for b in range(B):
        h = np.zeros((TP, D), np.float32)
        h[NPAD:128] = meta
        h[128:] = x[b]
        h0s.append(h)
    if FUSED:
        nc = _get_nc(DEPTH, True)
        li = _layer_inputs(*args, ls=range(DEPTH))
        in_maps = [dict(h0=h0s[b], **li, **consts) for b in range(B)]
        res = run_bass_kernel_spmd(nc, in_maps, core_ids=list(range(B)))
        return np.stack([np.asarray(r["out"], np.float32) for r in res.results], axis=0)
    nc = _get_nc(1, False)
    hs = h0s
    for l in range(DEPTH):
        li = _layer_inputs(*args, ls=[l])
        in_maps = [dict(h0=hs[b], **li, **consts) for b in range(B)]
        res = run_bass_kernel_spmd(nc, in_maps, core_ids=list(range(B)))
        hs = [np.asarray(r["hout"], np.float32) for r in res.results]
    return np.stack([h[128:] for h in hs], axis=0)
```
